# Optimizing a Trainium2 kernel written in Bass

```python
import math
import jax
import jax.numpy as jnp
from jax import lax
import numpy as np

D_MODEL = 1024
BATCH = 16
SEQ = 2048
DEPTH = 4

CTX_LEN = 256
GRID_W = 64
HEAD_DIM = 64
NA_HEADS = D_MODEL // (2 * HEAD_DIM)
NA_KH = 8
NA_KW = 16
DIFF_HEADS = D_MODEL // (4 * HEAD_DIM)
DIFF_VDIM = 2 * HEAD_DIM
GQA_Q_HEADS = D_MODEL // HEAD_DIM
GQA_KV_HEADS = 4
GQA_GROUP = GQA_Q_HEADS // GQA_KV_HEADS
D_FF = 2816
CONV_WIDTH = 3
Q_BLOCK = 128
ROPE_BASE = 10000.0
EPS = 1e-6
N_EVEN = (DEPTH + 1) // 2
N_ODD = DEPTH // 2
ALPHA = (2 * DEPTH) ** 0.25
BETA = (8 * DEPTH) ** -0.25
NA_W = NA_HEADS * HEAD_DIM
DIFF_W = DIFF_HEADS * 2 * HEAD_DIM
GQA_QW = GQA_Q_HEADS * HEAD_DIM
GQA_KW = GQA_KV_HEADS * HEAD_DIM

kernel_name = 'hybrid_natten_diff_gqa_convffn_dit'


def layer_norm(x, g, b):
    xf = x.astype(jnp.float32)
    mu = jnp.mean(xf, -1, keepdims=True)
    var = jnp.mean(jnp.square(xf - mu), -1, keepdims=True)
    return ((xf - mu) * lax.rsqrt(var + EPS)).astype(x.dtype) * g + b


def rms_norm(x, g):
    xf = x.astype(jnp.float32)
    return (xf * lax.rsqrt(jnp.mean(xf * xf, -1, keepdims=True) + EPS)).astype(x.dtype) * g


def softmax_f32(s):
    return jax.nn.softmax(s.astype(jnp.float32), axis=-1)


def modulate(x, shift, scale):
    return x * (1.0 + scale) + shift


def heads(t, n):
    b, t_len, _ = t.shape
    return t.reshape(b, t_len, n, -1).transpose(0, 2, 1, 3)


def merge_heads(t):
    b, n, t_len, d = t.shape
    return t.transpose(0, 2, 1, 3).reshape(b, t_len, n * d)


def rope_2d_tables(n_tokens, dtype):
    n_freq = HEAD_DIM // 4
    inv = ROPE_BASE ** (-jnp.arange(n_freq, dtype=jnp.float32) / n_freq)
    t = jnp.arange(n_tokens)
    row = (t // GRID_W).astype(jnp.float32)
    col = (t % GRID_W).astype(jnp.float32)
    ar = row[:, None] * inv
    ac = col[:, None] * inv
    ang = jnp.concatenate([ar, ar, ac, ac], -1)
    return jnp.cos(ang).astype(dtype), jnp.sin(ang).astype(dtype)


def apply_rope_2d(x, cos, sin):
    x1, x2, x3, x4 = jnp.split(x, 4, axis=-1)
    rot = jnp.concatenate([-x2, x1, -x4, x3], -1)
    return x * cos + rot * sin


def block_attention(q, k, v, scale):
    b, hk, g, s, d = q.shape
    nb = s // Q_BLOCK
    qb = jnp.moveaxis(q.reshape(b, hk, g, nb, Q_BLOCK, d), 3, 0)

    def one_block(qi):
        p = softmax_f32(jnp.einsum('bhgqd,bhkd->bhgqk', qi, k).astype(jnp.float32) * scale)
        return jnp.einsum('bhgqk,bhkd->bhgqd', p.astype(v.dtype), v)

    out = lax.map(one_block, qb)
    return jnp.moveaxis(out, 0, 3).reshape(b, hk, g, s, v.shape[-1])


def diff_block_attention(q1, q2, k1, k2, v, lam, scale):
    b, h, s, d = q1.shape
    nb = s // Q_BLOCK

    def to_blocks(t):
        return jnp.moveaxis(t.reshape(b, h, nb, Q_BLOCK, d), 2, 0)

    def one_block(qs):
        a1, a2 = qs
        p1 = softmax_f32(jnp.einsum('bhqd,bhkd->bhqk', a1, k1).astype(jnp.float32) * scale)
        p2 = softmax_f32(jnp.einsum('bhqd,bhkd->bhqk', a2, k2).astype(jnp.float32) * scale)
        return jnp.einsum('bhqk,bhkd->bhqd', (p1 - lam * p2).astype(v.dtype), v)

    out = lax.map(one_block, (to_blocks(q1), to_blocks(q2)))
    return jnp.moveaxis(out, 0, 2).reshape(b, h, s, v.shape[-1])


def neighbourhood_attention(q, k, v, kc, vc, rpb, scale):
    b, h, s, d = q.shape
    rows = s // GRID_W
    kh = min(NA_KH, rows)
    r = jnp.arange(rows)
    c = jnp.arange(GRID_W)
    row_start = jnp.clip(r - kh // 2, 0, rows - kh)
    col_start = jnp.clip(c - NA_KW // 2, 0, GRID_W - NA_KW)
    col_ok = (c[None, :] >= col_start[:, None]) & (c[None, :] < col_start[:, None] + NA_KW)
    dc_idx = jnp.clip(c[None, :] - c[:, None] + NA_KW - 1, 0, 2 * NA_KW - 2)
    bias_col = jnp.where(col_ok, rpb[:, :, dc_idx].astype(jnp.float32), -jnp.inf)
    k_grid = k.reshape(b, h, rows, GRID_W, d)
    v_grid = v.reshape(b, h, rows, GRID_W, d)
    q_rows = jnp.moveaxis(q.reshape(b, h, rows, GRID_W, d), 2, 0)
    n_loc = kh * GRID_W

    def one_row(args):
        qr, ri, rs = args
        kr = lax.dynamic_slice_in_dim(k_grid, rs, kh, axis=2)
        vr = lax.dynamic_slice_in_dim(v_grid, rs, kh, axis=2)
        bias = bias_col[:, rs + jnp.arange(kh) - ri + NA_KH - 1]
        s_loc = jnp.einsum('bhqd,bhikd->bhqik', qr, kr).astype(jnp.float32) * scale \
            + jnp.transpose(bias, (0, 2, 1, 3))[None]
        s_ctx = jnp.einsum('bhqd,bhld->bhql', qr, kc).astype(jnp.float32) * scale
        p = softmax_f32(jnp.concatenate([s_loc.reshape(b, h, GRID_W, n_loc), s_ctx], -1)).astype(v.dtype)
        return (jnp.einsum('bhqk,bhkd->bhqd', p[..., :n_loc], vr.reshape(b, h, n_loc, d))
                + jnp.einsum('bhql,bhld->bhqd', p[..., n_loc:], vc))

    out = lax.map(one_row, (q_rows, r, row_start))
    return jnp.moveaxis(out, 0, 2).reshape(b, h, s, d)


def mixer_ab(h, hc, w_in, w_o, rpb, lam_vec, subln_g, lambda_init, rope, need_ctx):
    cos, sin = rope
    scale = HEAD_DIM ** -0.5
    lv = lam_vec.astype(jnp.float32)
    lam = jnp.exp(jnp.sum(lv[0] * lv[1])) - jnp.exp(jnp.sum(lv[2] * lv[3])) + lambda_init
    splits = [NA_W, 2 * NA_W, 3 * NA_W, 3 * NA_W + DIFF_W, 3 * NA_W + 2 * DIFF_W]

    def project(t):
        nq, nk, nv, dq, dk, dv = jnp.split(t @ w_in, splits, axis=-1)
        return (heads(nq, NA_HEADS), heads(nk, NA_HEADS), heads(nv, NA_HEADS),
                heads(dq, 2 * DIFF_HEADS), heads(dk, 2 * DIFF_HEADS), heads(dv, DIFF_HEADS))

    def diff_post(o):
        return rms_norm(o, subln_g) * (1.0 - lambda_init)

    def out_proj(na_o, diff_o):
        return jnp.concatenate([merge_heads(na_o), merge_heads(diff_o)], -1) @ w_o

    nq, nk, nv, dq, dk, dv = project(h)
    cnq, cnk, cnv, cdq, cdk, cdv = project(hc)
    na = neighbourhood_attention(nq, nk, nv, cnk, cnv, rpb, scale)
    dq = apply_rope_2d(dq, cos, sin)
    dk = apply_rope_2d(dk, cos, sin)
    k1 = jnp.concatenate([dk[:, 0::2], cdk[:, 0::2]], axis=2)
    k2 = jnp.concatenate([dk[:, 1::2], cdk[:, 1::2]], axis=2)
    v_all = jnp.concatenate([dv, cdv], axis=2)
    diff = diff_post(diff_block_attention(dq[:, 0::2], dq[:, 1::2], k1, k2, v_all, lam, scale))
    y = out_proj(na, diff)
    if not need_ctx:
        return y, None
    na_c = block_attention(cnq[:, :, None], cnk, cnv, scale)[:, :, 0]
    diff_c = diff_post(diff_block_attention(cdq[:, 0::2], cdq[:, 1::2], cdk[:, 0::2], cdk[:, 1::2], cdv, lam, scale))
    return y, out_proj(na_c, diff_c)


def mixer_c(h, hc, w_in, w_o, qk_g, rope, need_ctx):
    cos, sin = rope
    scale = HEAD_DIM ** -0.5

    def project(t):
        b, t_len, _ = t.shape
        q, k, v = jnp.split(t @ w_in, [GQA_QW, GQA_QW + GQA_KW], axis=-1)
        q = rms_norm(heads(q, GQA_Q_HEADS), qk_g[0]).reshape(b, GQA_KV_HEADS, GQA_GROUP, t_len, HEAD_DIM)
        k = rms_norm(heads(k, GQA_KV_HEADS), qk_g[1])
        return q, k, heads(v, GQA_KV_HEADS)

    def out_proj(o):
        b, _, _, t_len, _ = o.shape
        return merge_heads(o.reshape(b, GQA_Q_HEADS, t_len, HEAD_DIM)) @ w_o

    q, k, v = project(h)
    cq, ck, cv = project(hc)
    q = apply_rope_2d(q, cos, sin)
    k = apply_rope_2d(k, cos, sin)
    y = out_proj(block_attention(q, jnp.concatenate([k, ck], 2), jnp.concatenate([v, cv], 2), scale))
    if not need_ctx:
        return y, None
    return y, out_proj(block_attention(cq, ck, cv, scale))


def conv_ffn(h, w_up, conv_w, conv_b, w_down):
    t_len = h.shape[1]
    pad = CONV_WIDTH // 2
    up = jnp.pad(h @ w_up, ((0, 0), (pad, pad), (0, 0)))
    u = sum(up[:, j:j + t_len] * conv_w[j] for j in range(CONV_WIDTH)) + conv_b
    a, g = jnp.split(u, 2, axis=-1)
    return (jax.nn.gelu(g) * a) @ w_down


def setup_inputs(seed: int = 0) -> dict:
    key = jax.random.key(seed)
    ks = jax.random.split(key, 32)
    f32 = jnp.float32
    D = D_MODEL

    def nrm(k, shape, s):
        return jax.random.normal(k, shape, f32) * s

    sd = D ** -0.5
    return {
        'x': nrm(ks[0], (BATCH, SEQ, D), 1.0),
        'c': nrm(ks[1], (BATCH, D), 1.0),
        'ctx': nrm(ks[2], (BATCH, CTX_LEN, D), 1.0),
        'c_ctx': nrm(ks[3], (D,), 1.0),
        'w_ada': nrm(ks[4], (DEPTH, D, 6 * D), sd),
        'b_ada': nrm(ks[5], (DEPTH, 6 * D), 0.02),
        'ln_g': 1.0 + nrm(ks[6], (DEPTH, 2, D), 0.02),
        'ln_b': nrm(ks[7], (DEPTH, 2, D), 0.02),
        'w_in_ab': jnp.concatenate([
            nrm(ks[8], (N_EVEN, D, 2 * NA_W), sd),
            nrm(ks[9], (N_EVEN, D, NA_W), BETA * sd),
            nrm(ks[10], (N_EVEN, D, 2 * DIFF_W), sd),
            nrm(ks[11], (N_EVEN, D, DIFF_HEADS * DIFF_VDIM), BETA * sd)], axis=-1),
        'w_o_ab': nrm(ks[12], (N_EVEN, NA_W + DIFF_HEADS * DIFF_VDIM, D), BETA * sd),
        'na_rpb': nrm(ks[13], (N_EVEN, NA_HEADS, 2 * NA_KH - 1, 2 * NA_KW - 1), 0.1),
        'diff_lambda': nrm(ks[14], (N_EVEN, 4, HEAD_DIM), 0.1),
        'diff_subln': 1.0 + nrm(ks[15], (N_EVEN, DIFF_VDIM), 0.02),
        'w_in_c': jnp.concatenate([
            nrm(ks[16], (N_ODD, D, GQA_QW + GQA_KW), sd),
            nrm(ks[17], (N_ODD, D, GQA_KW), BETA * sd)], axis=-1),
        'w_o_c': nrm(ks[18], (N_ODD, GQA_QW, D), BETA * GQA_QW ** -0.5),
        'gqa_qk_norm': 1.0 + nrm(ks[19], (N_ODD, 2, HEAD_DIM), 0.02),
        'w_up': nrm(ks[20], (DEPTH, D, 2 * D_FF), BETA * sd),
        'conv_w': nrm(ks[21], (DEPTH, CONV_WIDTH, 2 * D_FF), CONV_WIDTH ** -0.5),
        'conv_b': nrm(ks[22], (DEPTH, 2 * D_FF), 0.02),
        'w_down': nrm(ks[23], (DEPTH, D_FF, D), BETA * D_FF ** -0.5),
    }


def reference(x, c, ctx, c_ctx, w_ada, b_ada, ln_g, ln_b, w_in_ab, w_o_ab, na_rpb, diff_lambda,
              diff_subln, w_in_c, w_o_c, gqa_qk_norm, w_up, conv_w, conv_b, w_down):
    rope = rope_2d_tables(x.shape[1], x.dtype)
    cond = jax.nn.silu(c)
    cond_ctx = jax.nn.silu(c_ctx)
    xc = ctx
    for l in range(DEPTH):
        need_ctx = l < DEPTH - 1
        mod = jnp.split((cond @ w_ada[l] + b_ada[l])[:, None, :], 6, axis=-1)
        mod_c = jnp.split(cond_ctx @ w_ada[l] + b_ada[l], 6, axis=-1)
        h = modulate(x, mod[0], mod[1])
        hc = modulate(xc, mod_c[0], mod_c[1])
        if l % 2 == 0:
            i = l // 2
            lambda_init = 0.8 - 0.6 * math.exp(-0.3 * l)
            y, yc = mixer_ab(h, hc, w_in_ab[i], w_o_ab[i], na_rpb[i], diff_lambda[i], diff_subln[i],
                             lambda_init, rope, need_ctx)
        else:
            i = l // 2
            y, yc = mixer_c(h, hc, w_in_c[i], w_o_c[i], gqa_qk_norm[i], rope, need_ctx)
        x = layer_norm(ALPHA * x + mod[2] * y, ln_g[l, 0], ln_b[l, 0])
        f = conv_ffn(modulate(x, mod[3], mod[4]), w_up[l], conv_w[l], conv_b[l], w_down[l])
        x = layer_norm(ALPHA * x + mod[5] * f, ln_g[l, 1], ln_b[l, 1])
        if need_ctx:
            xc = layer_norm(ALPHA * xc + mod_c[2] * yc, ln_g[l, 0], ln_b[l, 0])
            fc = conv_ffn(modulate(xc, mod_c[3], mod_c[4]), w_up[l], conv_w[l], conv_b[l], w_down[l])
            xc = layer_norm(ALPHA * xc + mod_c[5] * fc, ln_g[l, 1], ln_b[l, 1])
    return x
```

```python
import math
import os
import numpy as np
import concourse.bass as bass
import concourse.mybir as mybir
from concourse.bass_utils import run_bass_kernel_spmd

F32 = mybir.dt.float32
BF16 = mybir.dt.bfloat16
ALU = mybir.AluOpType
AF = mybir.ActivationFunctionType
AX = mybir.AxisListType

D = 1024
S = 2048
L = 256
T = S + L
NCH = 8
DEPTH = 4
DFF = 2816
NJ = 22
EPS = 1e-6
ALPHA = (2 * DEPTH) ** 0.25
EPS_LN = EPS / (ALPHA * ALPHA)
NEG = -30000.0
SCALE = 0.125
GC = 1.5957691216057308

NA_TILES = {0: list(range(0, 6)), 1: list(range(2, 10)), 2: list(range(6, 14)), 3: list(range(10, 16))}
NA_TID0 = {0: 0, 1: 6, 2: 6, 3: 14}
NA_NT = 20


class Res:
    __slots__ = ("name", "w", "r", "dsem", "dn")

    def __init__(self, name):
        self.name = name
        self.w = None
        self.r = {}
        self.dsem = None
        self.dn = 0


class K:
    def __init__(self, nc):
        self.nc = nc
        self.E = {"pe": nc.tensor, "act": nc.scalar, "dve": nc.vector, "pool": nc.gpsimd, "sp": nc.sync}
        self.sem = {e: nc.alloc_semaphore("s_" + e) for e in ("pe", "act", "dve")}
        self.cnt = {e: 0 for e in ("pe", "act", "dve")}
        self.seen = {e: {} for e in self.E}
        self.nwait = 0
        self.nins = 0

    def _waits(self, eng, reads, writes):
        deps = {}
        for b in reads:
            if b.w is not None:
                s, v = b.w
                if deps.get(s, 0) < v:
                    deps[s] = v
        for b in writes:
            if b.w is not None:
                s, v = b.w
                if deps.get(s, 0) < v:
                    deps[s] = v
            for s, v in b.r.items():
                if deps.get(s, 0) < v:
                    deps[s] = v
        E = self.E[eng]
        seen = self.seen[eng]
        for s, v in deps.items():
            if eng == "pe" and s is self.sem["pe"]:
                continue
            if seen.get(s, 0) < v:
                E.wait_ge(s, v)
                seen[s] = v
                self.nwait += 1

    def _record(self, ev, reads, writes):
        s, v = ev
        for b in reads:
            if b.r.get(s, 0) < v:
                b.r[s] = v
        for b in writes:
            b.w = ev
            b.r = {}

    def op(self, eng, fn, reads=(), writes=(), inc=True):
        self._waits(eng, reads, writes)
        ins = fn(self.E[eng])
        self.nins += 1
        if inc:
            self.cnt[eng] += 1
            ins.then_inc(self.sem[eng], 1)
            ev = (self.sem[eng], self.cnt[eng])
        else:
            ev = (self.sem[eng], self.cnt[eng] + 1)
        self._record(ev, reads, writes)

    def dma(self, eng, out, in_, reads=(), writes=(), dst=None, ev_override=None):
        self._waits(eng, reads, writes)
        ins = self.E[eng].dma_start(out=out, in_=in_)
        self.nins += 1
        r = dst if dst is not None else writes[0]
        if r.dsem is None:
            r.dsem = self.nc.alloc_semaphore("d_" + r.name)
        r.dn += 16
        ins.then_inc(r.dsem, 16)
        ev = ev_override if ev_override is not None else (r.dsem, r.dn)
        self._record(ev, reads, writes)
        return ev


def blocks5():
    return [(i * 512, 512) for i in range(4)] + [(S, L)]


def build_program(nb, nlayers, last_stage, debug_ctx=False):
    nc = bass.Bass("TRN2", target_bir_lowering=False)
    k = K(nc)

    def din(name, shape, dt=F32):
        return nc.dram_tensor(name, list(shape), dt, kind="ExternalInput").ap()

    xT_d = din("xT", [nb, 128, NCH, T])
    cT_d = din("cT", [128, NCH, 3])
    wada_d = din("w_ada", [DEPTH * 48, 128, NCH * 128])
    bada_d = din("b_adaT", [128, DEPTH * 48])
    lng_d = din("ln_gT", [128, DEPTH * 2 * NCH])
    lnb_d = din("ln_bT", [128, DEPTH * 2 * NCH])
    winab_d = din("w_in_ab", [2 * 24, 128, NCH * 128])
    woab_d = din("w_o_ab", [2 * 8, 128, NCH * 128])
    winc_d = din("w_in_c", [2 * 12, 128, NCH * 128])
    woc_d = din("w_o_c", [2 * 8, 128, NCH * 128])
    wup_d = din("w_up", [DEPTH * 44, 128, NCH * 128])
    wdn_d = din("w_down", [DEPTH * 8, 128, NJ * 128])
    conv_d = din("convT", [128, DEPTH * 44 * 4])
    nab_d = din("na_bias", [2 * 8 * NA_NT, 128, 512])
    cos_d = din("cosT", [128, S])
    sin_d = din("sinT", [128, S])
    perm_d = din("permT", [128, 128])
    lam_d = din("lamv", [128, 2 * 256])
    sub_d = din("sublnT", [128, 2])
    qkg_d = din("qkgT", [128, 4])
    out_d = nc.dram_tensor("outT", [nb, 128, NCH, T if debug_ctx else S], F32, kind="ExternalOutput").ap()

    def sb(name, shape, dt):
        return nc.alloc_sbuf_tensor(name, list(shape), dt)

    XT = sb("XT", [128, NCH, T], F32)
    HT = sb("HT", [128, NCH, T], BF16)
    OT = sb("OT", [128, NCH * T], BF16)
    QKV = sb("QKV", [128, 4 * T], BF16)
    QU = QKV[:, 0:T]
    KU = QKV[:, T:2 * T]
    VA = QKV[:, 2 * T:4 * T].rearrange("p (t s d) -> p t s d", t=18, s=2)
    COS = sb("COS", [128, S], F32)
    SIN = sb("SIN", [128, S], F32)
    NTMP = 6
    TMP = [sb(f"TMP{i}", [128, 512], F32) for i in range(NTMP)]
    BADA = TMP[4]
    LAMV = TMP[3]
    NPT = 2
    PT = [sb(f"PT{i}", [128, 1024], BF16) for i in range(NPT)]
    NW = 4
    WT = [sb(f"WT{i}", [128, NCH, 128], BF16) for i in range(NW)]
    NWD = 2
    WD = [QKV[:, 0:NJ * 128].rearrange("p (j o) -> p j o", j=NJ),
          QKV[:, 2 * T:2 * T + NJ * 128].rearrange("p (j o) -> p j o", j=NJ)]
    MOD = sb("MOD", [128, DEPTH * 6 * NCH * 3], F32)
    LNG = sb("LNG", [128, DEPTH * 2 * NCH], F32)
    LNB = sb("LNB", [128, DEPTH * 2 * NCH], F32)
    CONV = sb("CONV", [128, 44 * 4], F32)
    CT = sb("CT", [128, NCH, 3], F32)
    CONDT = sb("CONDT", [128, NCH, 3], F32)
    PERM = sb("PERM", [128, 128], F32)
    ONESF = sb("ONESF", [128, 128], F32)
    BLK1 = sb("BLK1", [128, 128], F32)
    ONESB = sb("ONESB", [128, 128], BF16)
    SUBG = sb("SUBG", [128, 2], F32)
    QKG = sb("QKG", [128, 4], F32)
    SM = sb("SM", [128, 64], F32)
    PSA = nc.alloc_psum_tensor("PSA", [128, 1024], F32)
    PSB = nc.alloc_psum_tensor("PSB", [128, 1024], F32)
    PS = [PSA[:, 0:512], PSA[:, 512:1024], PSB[:, 0:512], PSB[:, 512:1024]] + \
         [nc.alloc_psum_tensor(f"PS{i}", [128, 512], F32) for i in range(4, 8)]

    rXT = [Res(f"XT{c}") for c in range(NCH)]
    rHT = [Res(f"HT{c}") for c in range(NCH)]
    rOT = [Res(f"OT{c}") for c in range(NCH)]
    rQU, rKU, rVA = Res("QU"), Res("KU"), Res("VA")
    rTMP = [Res(f"TMP{i}") for i in range(NTMP)]
    rPT = [Res(f"PT{i}") for i in range(NPT)]
    rWT = [Res(f"WT{i}") for i in range(NW)]
    rCONV = Res("CONV")
    rPS = [Res(f"PS{i}") for i in range(8)]
    rMOD, rCONST, rSM = Res("MOD"), Res("CONST"), Res("SM")
    rST = Res("STORE")
    ctr = {"tmp": 0, "pt": 0, "wt": 0, "wd": 0, "g": 0, "p": 0}

    def tmp():
        i = ctr["tmp"] % NTMP
        ctr["tmp"] += 1
        return TMP[i], rTMP[i]

    def ptile():
        i = ctr["pt"] % NPT
        ctr["pt"] += 1
        return PT[i], rPT[i]

    def gbank():
        i = 4 + ctr["g"] % 4
        ctr["g"] += 1
        return PS[i], rPS[i]

    def load_w(src_ap):
        i = ctr["wt"] % NW
        ctr["wt"] += 1
        k.dma("pool", WT[i][:].rearrange("p k o -> p (k o)"), src_ap, writes=[rWT[i]])
        return WT[i], rWT[i]

    for dst, src in ((LNG, lng_d), (LNB, lnb_d), (COS, cos_d), (SIN, sin_d),
                     (PERM, perm_d), (SUBG, sub_d), (QKG, qkg_d)):
        k.dma("sp", dst[:], src, writes=[rCONST])
    k.dma("sp", BADA[:, 0:DEPTH * 48], bada_d, writes=[rTMP[4]])
    k.dma("sp", LAMV[:, 0:512], lam_d, writes=[rTMP[3]])
    k.dma("sp", CT[:].rearrange("p k r -> p (k r)"), cT_d.rearrange("p k r -> p (k r)"), writes=[rCONST])
    k.op("dve", lambda e: e.memset(ONESF[:], 1.0), writes=[rCONST])
    k.op("dve", lambda e: e.memset(ONESB[:], 1.0), writes=[rCONST])
    k.op("dve", lambda e: e.memset(BLK1[:], 0.0), writes=[rCONST])
    k.op("dve", lambda e: e.memset(BLK1[0:64, 0:64], 1.0), writes=[rCONST])
    k.op("dve", lambda e: e.memset(BLK1[64:128, 64:128], 1.0), writes=[rCONST])
    k.op("dve", lambda e: e.tensor_scalar(out=QKG[:], in0=QKG[:], scalar1=8.0, scalar2=None, op0=ALU.mult),
         reads=[rCONST], writes=[rCONST])
    k.op("act", lambda e: e.activation(out=CONDT[:].rearrange("p k r -> p (k r)"),
                                       in_=CT[:].rearrange("p k r -> p (k r)"), func=AF.Silu),
         reads=[rCONST], writes=[rCONST])

    def midx(l, m, c, r):
        return ((l * 6 + m) * NCH + c) * 3 + r

    def modap(l, m, c, r):
        i = midx(l, m, c, r)
        return MOD[:, i:i + 1]

    for l in range(nlayers):
        for og in range(3):
            ps, rps = gbank()
            for j in range(16):
                oc = og * 16 + j
                c = oc % NCH
                wa = XT[:, c, 0:1024]
                k.dma("sp", wa, wada_d[l * 48 + oc], writes=[rXT[c]])
                wav = wa.rearrange("p (k o) -> p k o", k=NCH)
                for kc in range(NCH):
                    k.op("pe", lambda e, kc=kc, wav=wav, ps=ps, j=j: e.matmul(
                        ps[:, 3 * j:3 * j + 3], wav[:, kc, :], CONDT[:, kc, :], start=(kc == 0), stop=(kc == NCH - 1)),
                        reads=[rXT[c], rCONST], writes=[rps], inc=(kc == NCH - 1))
            for j in range(16):
                oc = og * 16 + j
                i0 = (l * 48 + oc) * 3
                k.op("dve", lambda e, ps=ps, j=j, i0=i0, oc=oc: e.tensor_scalar(
                    out=MOD[:, i0:i0 + 3], in0=ps[:, 3 * j:3 * j + 3], scalar1=BADA[:, l * 48 + oc:l * 48 + oc + 1],
                    scalar2=None, op0=ALU.add), reads=[rps, rTMP[4]], writes=[rMOD])
        for m, (op_, val) in ((1, (ALU.add, 1.0)), (4, (ALU.add, 1.0)), (2, (ALU.mult, 1.0 / ALPHA)),
                              (5, (ALU.mult, 1.0 / ALPHA))):
            a = midx(l, m, 0, 0)
            k.op("dve", lambda e, a=a, op_=op_, val=val: e.tensor_scalar(
                out=MOD[:, a:a + 24], in0=MOD[:, a:a + 24], scalar1=val, scalar2=None, op0=op_),
                reads=[rMOD], writes=[rMOD])

    for i in range(2):
        l = 2 * i
        if l >= nlayers:
            continue
        lam_init = 0.8 - 0.6 * math.exp(-0.3 * l)
        t0, rt0 = tmp()
        for a in range(2):
            k.op("dve", lambda e, a=a: e.tensor_tensor(
                out=t0[:, a * 64:(a + 1) * 64], in0=LAMV[:, i * 256 + a * 128:i * 256 + a * 128 + 64],
                in1=LAMV[:, i * 256 + a * 128 + 64:i * 256 + a * 128 + 128], op=ALU.mult),
                reads=[rTMP[3]], writes=[rt0])
            k.op("dve", lambda e, a=a: e.reduce_sum(out=SM[:, i * 8 + 1 + a:i * 8 + 2 + a],
                                                    in_=t0[:, a * 64:(a + 1) * 64], axis=AX.X),
                 reads=[rt0], writes=[rSM])
        k.op("act", lambda e: e.activation(out=SM[:, i * 8 + 3:i * 8 + 5], in_=SM[:, i * 8 + 1:i * 8 + 3], func=AF.Exp),
             reads=[rSM], writes=[rSM])
        k.op("dve", lambda e: e.tensor_tensor(out=SM[:, i * 8:i * 8 + 1], in0=SM[:, i * 8 + 4:i * 8 + 5],
                                              in1=SM[:, i * 8 + 3:i * 8 + 4], op=ALU.subtract),
             reads=[rSM], writes=[rSM])
        k.op("dve", lambda e: e.tensor_scalar(out=SM[:, i * 8:i * 8 + 1], in0=SM[:, i * 8:i * 8 + 1],
                                              scalar1=-lam_init, scalar2=None, op0=ALU.add),
             reads=[rSM], writes=[rSM])
        k.op("dve", lambda e: e.tensor_scalar(out=SUBG[:, i:i + 1], in0=SUBG[:, i:i + 1], scalar1=1.0 - lam_init,
                                              scalar2=None, op0=ALU.mult), reads=[rCONST], writes=[rCONST])

    def modulate_pass(l, m_shift, m_scale, b, with_ctx):
        for c in range(NCH):
            k.op("act", lambda e, c=c: e.activation(out=HT[:, c, 0:S], in_=XT[:, c, 0:S], func=AF.Identity,
                                                    bias=modap(l, m_shift, c, b), scale=modap(l, m_scale, c, b)),
                 reads=[rXT[c], rMOD], writes=[rHT[c]])
            if with_ctx:
                k.op("act", lambda e, c=c: e.activation(out=HT[:, c, S:T], in_=XT[:, c, S:T], func=AF.Identity,
                                                        bias=modap(l, m_shift, c, 2), scale=modap(l, m_scale, c, 2)),
                     reads=[rXT[c], rMOD], writes=[rHT[c]])

    def rstd_from(ps_ap, n, scale, bias, out_t, rout, rin):
        k.op("act", lambda e: e.activation(out=out_t[:, :n], in_=ps_ap, func=AF.Ln, bias=bias, scale=scale),
             reads=rin, writes=[rout])
        k.op("act", lambda e: e.activation(out=out_t[:, :n], in_=out_t[:, :n], func=AF.Exp, scale=-0.5),
             reads=[rout], writes=[rout])

    def pbank():
        i = ctr["p"] % 8
        ctr["p"] += 1
        return PS[i], rPS[i]

    def proj_fm(w_t, rw, mode, dst_t, rdst, g8=None, half=None):
        blks = blocks5()
        st = {}

        def stage1(bi):
            c0, n = blks[bi]
            ps, rps = pbank()
            if half is None:
                for kc in range(NCH):
                    k.op("pe", lambda e, kc=kc: e.matmul(ps[:, :n], w_t[:, kc, :], HT[:, kc, c0:c0 + n],
                                                         start=(kc == 0), stop=(kc == NCH - 1)),
                         reads=[rw, rHT[kc]], writes=[rps], inc=(kc == NCH - 1))
            else:
                for hh in range(2):
                    for kc in range(NCH):
                        k.op("pe", lambda e, kc=kc, hh=hh: e.matmul(
                            ps[hh * 64:(hh + 1) * 64, :n], w_t[:, kc, half * 64:(half + 1) * 64], HT[:, kc, c0:c0 + n],
                            start=(kc == 0), stop=(kc == NCH - 1), tile_position=(0, hh * 64)),
                            reads=[rw, rHT[kc]], writes=[rps], inc=(kc == NCH - 1 and hh == 1))
            if mode == "plain" or (mode == "rope" and c0 >= S):
                k.op("act", lambda e: e.activation(out=dst_t[:, c0:c0 + n], in_=ps[:, :n], func=AF.Copy),
                     reads=[rps], writes=[rdst])
                st[bi] = None
                return
            qf, rqf = tmp()
            d = dict(qf=qf, rqf=rqf, n=n, c0=c0)
            if mode == "gqa":
                k.op("act", lambda e: e.activation(out=qf[:, :n], in_=ps[:, :n], func=AF.Identity, scale=g8),
                     reads=[rps, rCONST], writes=[rqf])
                sq, rsq = tmp()
                k.op("act", lambda e: e.activation(out=sq[:, :n], in_=ps[:, :n], func=AF.Square),
                     reads=[rps], writes=[rsq])
                d.update(sq=sq, rsq=rsq)
            else:
                k.op("act", lambda e: e.activation(out=qf[:, :n], in_=ps[:, :n], func=AF.Copy),
                     reads=[rps], writes=[rqf])
            st[bi] = d

        def stage2(bi):
            d = st[bi]
            if d is None:
                return
            n = d["n"]
            if mode == "gqa":
                ss, rss = pbank()
                k.op("pe", lambda e: e.matmul(ss[:, :n], BLK1[:], d["sq"][:, :n], start=True, stop=True),
                     reads=[d["rsq"], rCONST], writes=[rss])
                d.update(ss=ss, rss=rss)
            if d["c0"] < S:
                rot, rrot = pbank()
                k.op("pe", lambda e: e.matmul(rot[:, :n], PERM[:], d["qf"][:, :n], start=True, stop=True),
                     reads=[d["rqf"], rCONST], writes=[rrot])
                d.update(rot=rot, rrot=rrot)

        def stage3(bi):
            d = st[bi]
            if d is None:
                return
            n, c0, qf, rqf = d["n"], d["c0"], d["qf"], d["rqf"]
            dst_ap = dst_t[:, c0:c0 + n]
            if mode == "gqa":
                ss, rss = d["ss"], d["rss"]
                R, rR = tmp()
                k.op("act", lambda e: e.activation(out=R[:, :n], in_=ss[:, :n], func=AF.Ln, bias=64.0 * EPS, scale=1.0),
                     reads=[rss], writes=[rR])
                k.op("act", lambda e: e.activation(out=R[:, :n], in_=R[:, :n], func=AF.Exp, scale=-0.5),
                     reads=[rR], writes=[rR])
                d.update(R=R, rR=rR)
            if c0 < S:
                if mode == "gqa":
                    B, rB = d["sq"], d["rsq"]
                else:
                    B, rB = tmp()
                k.op("dve", lambda e: e.tensor_tensor(out=B[:, :n], in0=d["rot"][:, :n], in1=SIN[:, c0:c0 + n], op=ALU.mult),
                     reads=[d["rrot"], rCONST], writes=[rB])
                k.op("dve", lambda e: e.tensor_tensor(out=qf[:, :n], in0=qf[:, :n], in1=COS[:, c0:c0 + n], op=ALU.mult),
                     reads=[rqf, rCONST], writes=[rqf])
                if mode == "gqa":
                    k.op("dve", lambda e: e.tensor_tensor(out=qf[:, :n], in0=qf[:, :n], in1=B[:, :n], op=ALU.add),
                         reads=[rqf, rB], writes=[rqf])
                    k.op("dve", lambda e: e.tensor_tensor(out=dst_ap, in0=qf[:, :n], in1=d["R"][:, :n], op=ALU.mult),
                         reads=[rqf, d["rR"]], writes=[rdst])
                else:
                    k.op("dve", lambda e: e.tensor_tensor(out=dst_ap, in0=qf[:, :n], in1=B[:, :n], op=ALU.add),
                         reads=[rqf, rB], writes=[rdst])
            else:
                k.op("dve", lambda e: e.tensor_tensor(out=dst_ap, in0=qf[:, :n], in1=d["R"][:, :n], op=ALU.mult),
                     reads=[rqf, d["rR"]], writes=[rdst])

        nblk = len(blks)
        for it in range(nblk + 1):
            if it < nblk:
                stage1(it)
            if it >= 1:
                stage2(it - 1)
                stage3(it - 1)

    def proj_v(w_t, rw, wc0, ncols, dst_fn):
        for t0 in range(0, 18, 4):
            nt = min(4, 18 - t0)
            ps, rps = gbank()
            for j in range(nt):
                t = t0 + j
                for kc in range(NCH):
                    k.op("pe", lambda e, kc=kc, t=t, j=j: e.matmul(
                        ps[:, j * ncols:(j + 1) * ncols], HT[:, kc, t * 128:(t + 1) * 128], w_t[:, kc, wc0:wc0 + ncols],
                        start=(kc == 0), stop=(kc == NCH - 1)),
                        reads=[rw, rHT[kc]], writes=[rps], inc=(kc == NCH - 1 and j == nt - 1))
            dst_fn(ps, rps, t0, nt)

    def attend_seq(jobs, mode, bias_fns=None):
        steps = [(ji, t) for ji, jb in enumerate(jobs) for t in range(len(jb["kt"]))]
        st = {}
        pts = {}

        def job_state(ji):
            if ji not in st:
                d = dict(accs=[gbank() for _ in range(2 if mode == "aug" else 4)])
                if mode == "diff":
                    d["dacc"] = tmp()
                st[ji] = d
            return st[ji]

        for g in range(len(steps) + 1):
            if g < len(steps):
                ji, t = steps[g]
                jb = jobs[ji]
                c0, nq = jb["c0"], jb["nq"]
                tok0, vaps, tid = jb["kt"][t]
                p = g % 2
                SP = PSA if p == 0 else PSB
                rS = [rPS[2 * p], rPS[2 * p + 1]]
                for s_ in range(2):
                    k.op("pe", lambda e, tok0=tok0, SP=SP, s_=s_, c0=c0, nq=nq: e.matmul(
                        SP[:, s_ * 512:s_ * 512 + nq], KU[64 * s_:64 * s_ + 64, tok0:tok0 + 128],
                        QU[64 * s_:64 * s_ + 64, c0:c0 + nq], start=True, stop=True),
                        reads=[rKU, rQU], writes=[rS[s_]], inc=(s_ == 1))
                P, rP = ptile()
                if bias_fns is not None and tid is not None:
                    for s_ in range(2):
                        bt, rbt = tmp()
                        k.dma("sp", bt[:, :nq], bias_fns[s_](tid), writes=[rbt])
                        k.op("dve", lambda e, SP=SP, bt=bt, s_=s_, nq=nq: e.scalar_tensor_tensor(
                            out=bt[:, :nq], in0=SP[:, s_ * 512:s_ * 512 + nq], scalar=SCALE, in1=bt[:, :nq],
                            op0=ALU.mult, op1=ALU.add), reads=[rS[s_], rbt], writes=[rbt])
                        k.op("act", lambda e, bt=bt, P=P, s_=s_, nq=nq: e.activation(
                            out=P[:, s_ * 512:s_ * 512 + nq], in_=bt[:, :nq], func=AF.Exp),
                            reads=[rbt], writes=[rP])
                else:
                    if nq == 512:
                        src, dst = SP[:, 0:1024], P[:, 0:1024]
                    else:
                        src = SP[:, 0:1024].rearrange("p (s n) -> p s n", s=2)[:, :, 0:nq]
                        dst = P[:, 0:1024].rearrange("p (s n) -> p s n", s=2)[:, :, 0:nq]
                    k.op("act", lambda e, src=src, dst=dst: e.activation(out=dst, in_=src, func=AF.Exp, scale=SCALE),
                         reads=rS, writes=[rP])
                pts[g] = (P, rP)
            if g >= 1:
                ji, tt = steps[g - 1]
                jb = jobs[ji]
                c0, nq = jb["c0"], jb["nq"]
                nt = len(jb["kt"])
                tok0, vaps, tid = jb["kt"][tt]
                P, rP = pts.pop(g - 1)
                last = (tt == nt - 1)
                d = job_state(ji)
                accs = d["accs"]
                Os = [accs[0], accs[1]] if mode == "aug" else [accs[0], accs[2]]
                for s_ in range(2):
                    O, rO = Os[s_]
                    k.op("pe", lambda e, vap=vaps[s_], P=P, O=O, s_=s_, nq=nq, tt=tt, last=last: e.matmul(
                        O[:, :nq], vap, P[:, s_ * 512:s_ * 512 + nq], start=(tt == 0), stop=last),
                        reads=[rVA, rP], writes=[rO], inc=last)
                if mode == "diff":
                    A, rA = d["dacc"]
                    if tt == 0:
                        k.op("dve", lambda e, P=P, A=A, nq=nq: e.tensor_copy(out=A[:, :nq], in_=P[:, 0:nq]),
                             reads=[rP], writes=[rA])
                    else:
                        k.op("dve", lambda e, P=P, A=A, nq=nq: e.tensor_tensor(
                            out=A[:, :nq], in0=A[:, :nq], in1=P[:, 0:nq], op=ALU.add), reads=[rP, rA], writes=[rA])
                    D1, rD1 = accs[3]
                    k.op("pe", lambda e, P=P, D1=D1, nq=nq, tt=tt, last=last: e.matmul(
                        D1[:, :nq], ONESB[:], P[:, 512:512 + nq], start=(tt == 0), stop=last),
                        reads=[rCONST, rP], writes=[rD1], inc=last)
                    if last:
                        D0, rD0 = accs[1]
                        k.op("pe", lambda e, A=A, D0=D0, nq=nq: e.matmul(D0[:, :nq], ONESF[:], A[:, :nq],
                                                                         start=True, stop=True),
                             reads=[rCONST, rA], writes=[rD0])
                if last:
                    jb["fin"](accs, nq, c0)

    def recip_act(dst_ap, src_ap, rsrc, rdst):
        k.op("act", lambda e: e.activation(out=dst_ap, in_=src_ap, func=AF.Ln), reads=rsrc, writes=[rdst])
        k.op("act", lambda e: e.activation(out=dst_ap, in_=dst_ap, func=AF.Exp, scale=-1.0), reads=[rdst], writes=[rdst])

    def aug_fin(ot_chunk, act_recip):
        def fin(accs, nq, c0):
            base = ot_chunk * T + c0
            for s_ in range(2):
                O, rO = accs[s_]
                rd, rrd = tmp()
                if act_recip:
                    recip_act(rd[64:128, :nq], O[64:128, :nq], [rO], rrd)
                else:
                    k.op("dve", lambda e, O=O, rd=rd: e.reciprocal(out=rd[64:128, :nq], in_=O[64:128, :nq]),
                         reads=[rO], writes=[rrd])
                k.op("dve", lambda e, O=O, rd=rd, s_=s_: e.tensor_tensor(
                    out=OT[64 * s_:64 * s_ + 64, base:base + nq], in0=O[0:64, :nq], in1=rd[64:128, :nq], op=ALU.mult),
                    reads=[rO, rrd], writes=[rOT[ot_chunk]])
        return fin

    def ctx_tiles(vfn0, vfn1):
        return [(S + j * 128, (vfn0(16 + j), vfn1(16 + j)), None) for j in range(2)]

    def res_ln_block(l, which, r, c0, n, ybank_fn, st1, rst1, st2, rst2):
        mg = 2 if which == 0 else 5
        for oc in range(NCH):
            yp, ryp = ybank_fn(oc)
            xs = XT[:, oc, c0:c0 + n]
            k.op("dve", lambda e, yp=yp, xs=xs, oc=oc: e.scalar_tensor_tensor(
                out=xs, in0=yp, scalar=modap(l, mg, oc, r), in1=xs, op0=ALU.mult, op1=ALU.add),
                reads=[ryp, rXT[oc], rMOD], writes=[rXT[oc]])
            sq, rsq = tmp()
            k.op("act", lambda e, xs=xs, sq=sq: e.activation(out=sq[:, :n], in_=xs, func=AF.Square),
                 reads=[rXT[oc]], writes=[rsq])
            k.op("pe", lambda e, xs=xs, oc=oc: e.matmul(st1[:, :n], ONESF[:], xs, start=(oc == 0), stop=(oc == NCH - 1)),
                 reads=[rXT[oc], rCONST], writes=[rst1], inc=False)
            k.op("pe", lambda e, sq=sq, oc=oc: e.matmul(st2[:, :n], ONESF[:], sq[:, :n], start=(oc == 0),
                                                        stop=(oc == NCH - 1)),
                 reads=[rsq, rCONST], writes=[rst2])
        mean, rmean = tmp()
        k.op("dve", lambda e: e.tensor_scalar(out=mean[:, :n], in0=st1[:, :n], scalar1=1.0 / D, scalar2=None,
                                              op0=ALU.mult), reads=[rst1], writes=[rmean])
        var, rvar = tmp()
        k.op("dve", lambda e: e.tensor_tensor(out=var[:, :n], in0=mean[:, :n], in1=mean[:, :n], op=ALU.mult),
             reads=[rmean], writes=[rvar])
        k.op("dve", lambda e: e.scalar_tensor_tensor(out=var[:, :n], in0=st2[:, :n], scalar=1.0 / D, in1=var[:, :n],
                                                     op0=ALU.mult, op1=ALU.subtract),
             reads=[rst2, rvar], writes=[rvar])
        rstd_from(var[:, :n], n, 1.0, EPS_LN, var, rvar, [rvar])
        k.op("dve", lambda e: e.scalar_tensor_tensor(out=mean[:, :n], in0=mean[:, :n], scalar=-1.0, in1=var[:, :n],
                                                     op0=ALU.mult, op1=ALU.mult),
             reads=[rmean, rvar], writes=[rmean])
        gi = (l * 2 + which) * NCH
        for oc in range(NCH):
            xs = XT[:, oc, c0:c0 + n]
            k.op("dve", lambda e, xs=xs: e.tensor_tensor(out=xs, in0=xs, in1=var[:, :n], op=ALU.mult),
                 reads=[rXT[oc], rvar], writes=[rXT[oc]])
            k.op("dve", lambda e, xs=xs: e.tensor_tensor(out=xs, in0=xs, in1=mean[:, :n], op=ALU.add),
                 reads=[rXT[oc], rmean], writes=[rXT[oc]])
            k.op("act", lambda e, xs=xs, oc=oc: e.activation(out=xs, in_=xs, func=AF.Identity,
                                                             bias=LNB[:, gi + oc:gi + oc + 1],
                                                             scale=LNG[:, gi + oc:gi + oc + 1]),
                 reads=[rXT[oc], rCONST], writes=[rXT[oc]])

    def mixer_even(l, b, need_ctx):
        i = l // 2
        k.op("dve", lambda e: e.memset(VA[:, :, :, 64:128], 1.0), writes=[rVA])
        qblocks = [(qb * 512, 512) for qb in range(4)] + ([(S, L)] if need_ctx else [])
        for u in range(4):
            wq, rwq = load_w(winab_d[i * 24 + u])
            proj_fm(wq, rwq, "plain", QU, rQU)
            wk, rwk = load_w(winab_d[i * 24 + 4 + u])
            proj_fm(wk, rwk, "plain", KU, rKU)
            wv, rwv = load_w(winab_d[i * 24 + 8 + u])

            def vdst(ps, rps, t0, nt):
                k.op("act", lambda e: e.activation(
                    out=VA[:, t0:t0 + nt, :, 0:64],
                    in_=ps[:, 0:nt * 128].rearrange("p (t s d) -> p t s d", t=nt, s=2), func=AF.Copy),
                    reads=[rps], writes=[rVA])
            proj_v(wv, rwv, 0, 128, vdst)
            vfn0 = lambda t: VA[:, t, 0, :]
            vfn1 = lambda t: VA[:, t, 1, :]
            bias_fns = tuple((lambda tid, h=2 * u + s_: nab_d[(i * 8 + h) * NA_NT + tid]) for s_ in range(2))
            jobs = []
            for qi, (c0, nq) in enumerate(qblocks):
                if c0 < S:
                    kt = [(tl * 128, (vfn0(tl), vfn1(tl)), NA_TID0[qi] + j) for j, tl in enumerate(NA_TILES[qi])]
                    kt += ctx_tiles(vfn0, vfn1)
                else:
                    kt = ctx_tiles(vfn0, vfn1)
                jobs.append(dict(c0=c0, nq=nq, kt=kt, fin=aug_fin(u, True)))
            attend_seq(jobs, "aug", bias_fns)
        for u in range(4):
            wq, rwq = load_w(winab_d[i * 24 + 12 + u])
            proj_fm(wq, rwq, "rope", QU, rQU)
            wk, rwk = load_w(winab_d[i * 24 + 16 + u])
            proj_fm(wk, rwk, "rope", KU, rKU)
            wv, rwv = load_w(winab_d[i * 24 + 20 + u])

            def vdst(ps, rps, t0, nt):
                k.op("act", lambda e: e.activation(
                    out=VA[:, t0:t0 + nt, 1, :], in_=ps[:, 0:nt * 128].rearrange("p (t d) -> p t d", t=nt),
                    func=AF.Copy), reads=[rps], writes=[rVA])
            proj_v(wv, rwv, 0, 128, vdst)
            vfn = lambda t: VA[:, t, 1, :]
            def diff_fin(accs, nq, c0, u=u):
                (O0, rO0), (D0, rD0), (O1, rO1), (D1, rD1) = accs
                r1, rr1 = tmp()
                recip_act(r1[:, :nq], D0[:, :nq], [rD0], rr1)
                r2, rr2 = tmp()
                recip_act(r2[:, :nq], D1[:, :nq], [rD1], rr2)
                a1, ra1 = r1, rr1
                k.op("dve", lambda e: e.tensor_tensor(out=a1[:, :nq], in0=O0[:, :nq], in1=r1[:, :nq], op=ALU.mult),
                     reads=[rO0, rr1], writes=[ra1])
                k.op("dve", lambda e: e.tensor_tensor(out=r2[:, :nq], in0=O1[:, :nq], in1=r2[:, :nq], op=ALU.mult),
                     reads=[rO1, rr2], writes=[rr2])
                k.op("dve", lambda e: e.scalar_tensor_tensor(out=a1[:, :nq], in0=r2[:, :nq], scalar=SM[:, i * 8:i * 8 + 1],
                                                             in1=a1[:, :nq], op0=ALU.mult, op1=ALU.add),
                     reads=[rr2, ra1, rSM], writes=[ra1])
                sq, rsq = tmp()
                k.op("act", lambda e: e.activation(out=sq[:, :nq], in_=a1[:, :nq], func=AF.Square),
                     reads=[ra1], writes=[rsq])
                ss, rss = gbank()
                k.op("pe", lambda e: e.matmul(ss[:, :nq], ONESF[:], sq[:, :nq], start=True, stop=True),
                     reads=[rsq, rCONST], writes=[rss])
                rstd_from(ss[:, :nq], nq, 1.0 / 128.0, EPS, sq, rsq, [rss])
                base = (4 + u) * T + c0
                k.op("dve", lambda e: e.scalar_tensor_tensor(out=OT[:, base:base + nq], in0=a1[:, :nq],
                                                             scalar=SUBG[:, i:i + 1], in1=sq[:, :nq],
                                                             op0=ALU.mult, op1=ALU.mult),
                     reads=[ra1, rsq, rCONST], writes=[rOT[4 + u]])
            jobs = []
            for (c0, nq) in qblocks:
                if c0 < S:
                    kt = [(tl * 128, (vfn(tl), vfn(tl)), None) for tl in range(18)]
                else:
                    kt = ctx_tiles(vfn, vfn)
                jobs.append(dict(c0=c0, nq=nq, kt=kt, fin=diff_fin))
            attend_seq(jobs, "diff")
        return woab_d, i * 8

    def mixer_odd(l, b, need_ctx):
        i = l // 2
        k.op("dve", lambda e: e.memset(VA[:, :, 0, 64:128], 1.0), writes=[rVA])
        qblocks = [(qb * 512, 512) for qb in range(4)] + ([(S, L)] if need_ctx else [])
        for u in range(8):
            g = u // 2
            wq, rwq = load_w(winc_d[i * 12 + u])
            proj_fm(wq, rwq, "gqa", QU, rQU, g8=QKG[:, 2 * i:2 * i + 1])
            if u % 2 == 0:
                wk, rwk = load_w(winc_d[i * 12 + 8 + g // 2])
                proj_fm(wk, rwk, "gqa", KU, rKU, g8=QKG[:, 2 * i + 1:2 * i + 2], half=g % 2)
                wv, rwv = load_w(winc_d[i * 12 + 10 + g // 2])

                def vdst(ps, rps, t0, nt):
                    k.op("act", lambda e: e.activation(
                        out=VA[:, t0:t0 + nt, 0, 0:64], in_=ps[:, 0:nt * 64].rearrange("p (t d) -> p t d", t=nt),
                        func=AF.Copy), reads=[rps], writes=[rVA])
                proj_v(wv, rwv, (g % 2) * 64, 64, vdst)
            vfn = lambda t: VA[:, t, 0, :]
            jobs = []
            for (c0, nq) in qblocks:
                if c0 < S:
                    kt = [(tl * 128, (vfn(tl), vfn(tl)), None) for tl in range(18)]
                else:
                    kt = ctx_tiles(vfn, vfn)
                jobs.append(dict(c0=c0, nq=nq, kt=kt, fin=aug_fin(u, False)))
            attend_seq(jobs, "aug")
        return woc_d, i * 8

    def out_proj_ln(l, b, need_ctx, wo_d, wo_base):
        blks = blocks5() if need_ctx else blocks5()[:4]
        bigs = [blks[0:2], blks[2:4]] + ([blks[4:5]] if need_ctx else [])
        for big in bigs:
            for oc in range(NCH):
                w, rw = load_w(wo_d[wo_base + oc])
                for si, (c0, n) in enumerate(big):
                    r = b if c0 < S else 2
                    st1, rst1, st2, rst2 = PS[2 * si], rPS[2 * si], PS[2 * si + 1], rPS[2 * si + 1]
                    ps, rps = gbank()
                    for kc in range(NCH):
                        k.op("pe", lambda e, kc=kc: e.matmul(ps[:, :n], w[:, kc, :], OT[:, kc * T + c0:kc * T + c0 + n],
                                                             start=(kc == 0), stop=(kc == NCH - 1)),
                             reads=[rw, rOT[kc]], writes=[rps], inc=(kc == NCH - 1))
                    xs = XT[:, oc, c0:c0 + n]
                    k.op("dve", lambda e, xs=xs, r=r: e.scalar_tensor_tensor(
                        out=xs, in0=ps[:, :n], scalar=modap(l, 2, oc, r), in1=xs, op0=ALU.mult, op1=ALU.add),
                        reads=[rps, rXT[oc], rMOD], writes=[rXT[oc]])
                    sq, rsq = tmp()
                    k.op("act", lambda e, xs=xs, sq=sq: e.activation(out=sq[:, :n], in_=xs, func=AF.Square),
                         reads=[rXT[oc]], writes=[rsq])
                    k.op("pe", lambda e, xs=xs: e.matmul(st1[:, :n], ONESF[:], xs, start=(oc == 0), stop=(oc == NCH - 1)),
                         reads=[rXT[oc], rCONST], writes=[rst1], inc=False)
                    k.op("pe", lambda e, sq=sq: e.matmul(st2[:, :n], ONESF[:], sq[:, :n], start=(oc == 0),
                                                         stop=(oc == NCH - 1)),
                         reads=[rsq, rCONST], writes=[rst2])
            for si, (c0, n) in enumerate(big):
                ln_finish(l, 0, c0, n, PS[2 * si], rPS[2 * si], PS[2 * si + 1], rPS[2 * si + 1])

    def ffn(l, b, need_ctx):
        subs = [(0, 410, 0, S), (410, 410, 0, S), (820, 410, 0, S), (1230, 410, 0, S), (1640, 408, 0, S)]
        bigs = [[subs[0], subs[1]], [subs[2], subs[3]], [subs[4]]]
        if need_ctx:
            bigs[2].append((S, L, S, T))
        rACT = rOT
        k.dma("sp", CONV[:], conv_d[:, l * 176:(l + 1) * 176], writes=[rCONV])
        pending = [None]
        for big in bigs:
            for j in range(NJ):
                wa, rwa = load_w(wup_d[l * 44 + j])
                wg, rwg = load_w(wup_d[l * 44 + NJ + j])
                for si, (s0, n, q0, q1) in enumerate(big):
                    lo = 1 if s0 - 1 < q0 else 0
                    hi = n - 1 if s0 + n + 1 > q1 else n
                    ra0 = s0 - 1 + lo
                    ncol = (hi + 2) - lo
                    tts = []
                    for which, (w, rw) in enumerate(((wa, rwa), (wg, rwg))):
                        bi = (j % 2) * 4 + si * 2 + which
                        ps, rps = PS[bi], rPS[bi]
                        for kc in range(NCH):
                            k.op("pe", lambda e, kc=kc, w=w, ps=ps: e.matmul(
                                ps[:, lo:lo + ncol], w[:, kc, :], HT[:, kc, ra0:ra0 + ncol],
                                start=(kc == 0), stop=(kc == NCH - 1)),
                                reads=[rw, rHT[kc]], writes=[rps], inc=(kc == NCH - 1))
                        ci = (which * NJ + j) * 4
                        tt, rtt = tmp()
                        k.op("act", lambda e, ps=ps, tt=tt, ci=ci: e.activation(
                            out=tt[:, :n], in_=ps[:, 1:n + 1], func=AF.Identity, bias=CONV[:, ci + 3:ci + 4],
                            scale=CONV[:, ci + 1:ci + 2]), reads=[rps, rCONV], writes=[rtt])
                        k.op("dve", lambda e, ps=ps, tt=tt, ci=ci: e.scalar_tensor_tensor(
                            out=tt[:, lo:n], in0=ps[:, lo:n], scalar=CONV[:, ci:ci + 1], in1=tt[:, lo:n],
                            op0=ALU.mult, op1=ALU.add), reads=[rps, rtt, rCONV], writes=[rtt])
                        k.op("dve", lambda e, ps=ps, tt=tt, ci=ci: e.scalar_tensor_tensor(
                            out=tt[:, 0:hi], in0=ps[:, 2:hi + 2], scalar=CONV[:, ci + 2:ci + 3], in1=tt[:, 0:hi],
                            op0=ALU.mult, op1=ALU.add), reads=[rps, rtt, rCONV], writes=[rtt])
                        tts.append((tt, rtt))
                    (ta, rta), (tg, rtg) = tts
                    ab = (j * 2 + si) * 410

                    def fin(ta=ta, rta=rta, tg=tg, rtg=rtg, n=n, ab=ab, j=j):
                        k.op("act", lambda e: e.activation(out=tg[:, :n], in_=tg[:, :n], func=AF.Gelu_apprx_tanh),
                             reads=[rtg], writes=[rtg])
                        k.op("dve", lambda e: e.tensor_tensor(out=OT[:, ab:ab + n], in0=ta[:, :n], in1=tg[:, :n],
                                                              op=ALU.mult),
                             reads=[rta, rtg], writes=[rACT[j % NCH]])
                    if pending[0] is not None:
                        pending[0]()
                    pending[0] = fin
            if pending[0] is not None:
                pending[0]()
                pending[0] = None
            ybanks = {}
            for oc in range(NCH):
                iwd = ctr["wd"] % NWD
                ctr["wd"] += 1
                wdres = [rQU, rKU] if iwd == 0 else [rVA]
                wdsem = rQU if iwd == 0 else rVA
                k.dma("pool", (QKV[:, 0:NJ * 128] if iwd == 0 else QKV[:, 2 * T:2 * T + NJ * 128]), wdn_d[l * 8 + oc],
                      writes=wdres, dst=wdsem)
                for si, (s0, n, q0, q1) in enumerate(big):
                    bi = (oc % 2) * 2 + si
                    ps, rps = PS[bi], rPS[bi]
                    for j in range(NJ):
                        ab = (j * 2 + si) * 410
                        k.op("pe", lambda e, j=j, ab=ab, ps=ps, iwd=iwd: e.matmul(
                            ps[:, :n], WD[iwd][:, j, :], OT[:, ab:ab + n], start=(j == 0), stop=(j == NJ - 1)),
                            reads=wdres + [rACT[j % NCH]], writes=[rps], inc=(j == NJ - 1))
                    ybanks[(oc, si)] = (ps, rps)
                    r = b if s0 < S else 2
                    st1, rst1, st2, rst2 = PS[4 + 2 * si], rPS[4 + 2 * si], PS[5 + 2 * si], rPS[5 + 2 * si]
                    xs = XT[:, oc, s0:s0 + n]
                    k.op("dve", lambda e, ps=ps, xs=xs, oc=oc, r=r: e.scalar_tensor_tensor(
                        out=xs, in0=ps[:, :n], scalar=modap(l, 5, oc, r), in1=xs, op0=ALU.mult, op1=ALU.add),
                        reads=[rps, rXT[oc], rMOD], writes=[rXT[oc]])
                    sq, rsq = tmp()
                    k.op("act", lambda e, xs=xs, sq=sq: e.activation(out=sq[:, :n], in_=xs, func=AF.Square),
                         reads=[rXT[oc]], writes=[rsq])
                    k.op("pe", lambda e, xs=xs, oc=oc, st1=st1: e.matmul(st1[:, :n], ONESF[:], xs, start=(oc == 0),
                                                                         stop=(oc == NCH - 1)),
                         reads=[rXT[oc], rCONST], writes=[rst1], inc=False)
                    k.op("pe", lambda e, sq=sq, oc=oc, st2=st2: e.matmul(st2[:, :n], ONESF[:], sq[:, :n], start=(oc == 0),
                                                                         stop=(oc == NCH - 1)),
                         reads=[rsq, rCONST], writes=[rst2])
            for si, (s0, n, q0, q1) in enumerate(big):
                st1, rst1, st2, rst2 = PS[4 + 2 * si], rPS[4 + 2 * si], PS[5 + 2 * si], rPS[5 + 2 * si]
                ln_finish(l, 1, s0, n, st1, rst1, st2, rst2)

    def ln_finish(l, which, c0, n, st1, rst1, st2, rst2):
        mean, rmean = tmp()
        k.op("dve", lambda e: e.tensor_scalar(out=mean[:, :n], in0=st1[:, :n], scalar1=1.0 / D, scalar2=None,
                                              op0=ALU.mult), reads=[rst1], writes=[rmean])
        var, rvar = tmp()
        k.op("dve", lambda e: e.tensor_tensor(out=var[:, :n], in0=mean[:, :n], in1=mean[:, :n], op=ALU.mult),
             reads=[rmean], writes=[rvar])
        k.op("dve", lambda e: e.scalar_tensor_tensor(out=var[:, :n], in0=st2[:, :n], scalar=1.0 / D, in1=var[:, :n],
                                                     op0=ALU.mult, op1=ALU.subtract),
             reads=[rst2, rvar], writes=[rvar])
        rstd_from(var[:, :n], n, 1.0, EPS_LN, var, rvar, [rvar])
        k.op("dve", lambda e: e.scalar_tensor_tensor(out=mean[:, :n], in0=mean[:, :n], scalar=-1.0, in1=var[:, :n],
                                                     op0=ALU.mult, op1=ALU.mult),
             reads=[rmean, rvar], writes=[rmean])
        gi = (l * 2 + which) * NCH
        for oc in range(NCH):
            xs = XT[:, oc, c0:c0 + n]
            k.op("dve", lambda e, xs=xs: e.tensor_tensor(out=xs, in0=xs, in1=var[:, :n], op=ALU.mult),
                 reads=[rXT[oc], rvar], writes=[rXT[oc]])
            k.op("dve", lambda e, xs=xs: e.tensor_tensor(out=xs, in0=xs, in1=mean[:, :n], op=ALU.add),
                 reads=[rXT[oc], rmean], writes=[rXT[oc]])
            k.op("act", lambda e, xs=xs, oc=oc: e.activation(out=xs, in_=xs, func=AF.Identity,
                                                             bias=LNB[:, gi + oc:gi + oc + 1],
                                                             scale=LNG[:, gi + oc:gi + oc + 1]),
                 reads=[rXT[oc], rCONST], writes=[rXT[oc]])

    nstore = 0
    for b in range(nb):
        for c in range(NCH):
            k.dma("sp", XT[:, c, :], xT_d[b, :, c, :], writes=[rXT[c]])
        for l in range(nlayers):
            need_ctx = l < DEPTH - 1
            modulate_pass(l, 0, 1, b, True)
            if l % 2 == 0:
                wo_d, wo_base = mixer_even(l, b, need_ctx)
            else:
                wo_d, wo_base = mixer_odd(l, b, need_ctx)
            out_proj_ln(l, b, need_ctx, wo_d, wo_base)
            if l == nlayers - 1 and last_stage == 1:
                break
            modulate_pass(l, 3, 4, b, need_ctx)
            ffn(l, b, need_ctx)
        nstore += NCH
        ncol_out = T if debug_ctx else S
        for c in range(NCH):
            k.dma("sp", out_d[b, :, c, :], XT[:, c, 0:ncol_out], reads=[rXT[c]], dst=rST,
                  ev_override=None)
        for c in range(NCH):
            rXT[c].r[rST.dsem] = rST.dn
    nc.sync.wait_ge(rST.dsem, rST.dn)
    print(f"[kernel] instructions={k.nins} waits={k.nwait} cnt={k.cnt}", flush=True)
    return nc


def _tile_w(w, ncol_tiles):
    kin = w.shape[0] // 128
    a = w.reshape(kin, 128, ncol_tiles, 128).transpose(2, 1, 0, 3)
    return np.ascontiguousarray(a).reshape(ncol_tiles, 128, kin * 128)


def _fm(v):
    sh = v.shape
    n = sh[-1] // 128
    a = v.reshape(sh[:-1] + (n, 128))
    a = np.moveaxis(a, -1, 0)
    return np.ascontiguousarray(a).reshape(128, -1)


def _na_bias(rpb):
    H = rpb.shape[0]
    c = np.arange(64)
    cs = np.clip(c - 8, 0, 48)
    col_ok = (c[None, :] >= cs[:, None]) & (c[None, :] < cs[:, None] + 16)
    dc = np.clip(c[None, :] - c[:, None] + 15, 0, 30)
    bc = np.where(col_ok[None, None], rpb[:, :, dc], np.float32(NEG)).astype(np.float32)
    bcT = np.ascontiguousarray(bc.transpose(0, 1, 3, 2))
    out = np.full((H, NA_NT, 128, 512), NEG, np.float32)
    for qb in (0, 1, 3):
        for j, tl in enumerate(NA_TILES[qb]):
            tid = NA_TID0[qb] + j
            for ii in range(2):
                kr = 2 * tl + ii
                for jj in range(8):
                    qr = 8 * qb + jj
                    rs = min(max(qr - 4, 0), 24)
                    if rs <= kr < rs + 8:
                        out[:, tid, ii * 64:(ii + 1) * 64, jj * 64:(jj + 1) * 64] = bcT[:, kr - qr + 7]
    return out


def _rope_tables():
    inv = (10000.0 ** (-np.arange(16, dtype=np.float32) / 16)).astype(np.float32)
    t = np.arange(S)
    row = (t // 64).astype(np.float32)
    col = (t % 64).astype(np.float32)
    ar = row[:, None] * inv
    ac = col[:, None] * inv
    ang = np.concatenate([ar, ar, ac, ac], -1).astype(np.float32)
    cos = np.cos(ang).astype(np.float32).T
    sin = np.sin(ang).astype(np.float32).T
    sign = np.ones(64, np.float32)
    sign[0:16] = -1
    sign[32:48] = -1
    sinS = sin * sign[:, None]
    perm = np.zeros((128, 128), np.float32)
    for blk in range(2):
        for d in range(64):
            src = d + 16 if (d < 16 or 32 <= d < 48) else d - 16
            perm[blk * 64 + src, blk * 64 + d] = 1.0
    return (np.ascontiguousarray(np.tile(cos, (2, 1))), np.ascontiguousarray(np.tile(sinS, (2, 1))), perm)


def _shared_inputs(inp):
    f = np.float32
    sh = {}
    sh["w_ada"] = np.concatenate([_tile_w(np.asarray(inp["w_ada"][l], f), 48) for l in range(DEPTH)], 0)
    sh["b_adaT"] = _fm(np.asarray(inp["b_ada"], f).reshape(-1))
    sh["ln_gT"] = _fm(np.asarray(inp["ln_g"], f).reshape(-1))
    sh["ln_bT"] = _fm(np.asarray(inp["ln_b"], f).reshape(-1))
    sh["w_in_ab"] = np.concatenate([_tile_w(np.asarray(inp["w_in_ab"][i], f), 24) for i in range(2)], 0)
    sh["w_o_ab"] = np.concatenate([_tile_w(np.asarray(inp["w_o_ab"][i], f), 8) for i in range(2)], 0)
    sh["w_in_c"] = np.concatenate([_tile_w(np.asarray(inp["w_in_c"][i], f), 12) for i in range(2)], 0)
    sh["w_o_c"] = np.concatenate([_tile_w(np.asarray(inp["w_o_c"][i], f), 8) for i in range(2)], 0)
    sh["w_up"] = np.concatenate([_tile_w(np.asarray(inp["w_up"][l], f), 44) for l in range(DEPTH)], 0)
    sh["w_down"] = np.concatenate([_tile_w(np.asarray(inp["w_down"][l], f), 8) for l in range(DEPTH)], 0)
    cw = np.asarray(inp["conv_w"], f)
    cb = np.asarray(inp["conv_b"], f)
    cv = np.concatenate([cw, cb[:, None, :]], 1)
    cv = cv.reshape(DEPTH, 4, 44, 128).transpose(3, 0, 2, 1)
    sh["convT"] = np.ascontiguousarray(cv).reshape(128, -1)
    sh["na_bias"] = np.concatenate([_na_bias(np.asarray(inp["na_rpb"][i], f)) for i in range(2)], 0).reshape(
        2 * 8 * NA_NT, 128, 512)
    cosT, sinT, perm = _rope_tables()
    sh["cosT"], sh["sinT"], sh["permT"] = cosT, sinT, perm
    lam = np.asarray(inp["diff_lambda"], f).reshape(1, 2 * 256)
    sh["lamv"] = np.ascontiguousarray(np.broadcast_to(lam, (128, 512)))
    sh["sublnT"] = np.ascontiguousarray(np.asarray(inp["diff_subln"], f).T)
    g = np.asarray(inp["gqa_qk_norm"], f).reshape(4, 64)
    sh["qkgT"] = np.ascontiguousarray(np.tile(g.T, (2, 1)))
    return sh


def _core_inputs(inp, core, nb):
    f = np.float32
    x = np.asarray(inp["x"], f)
    ctx = np.asarray(inp["ctx"], f)
    c = np.asarray(inp["c"], f)
    cc = np.asarray(inp["c_ctx"], f)
    bs = [core * 2 + j for j in range(nb)]
    xs = []
    for b_ in bs:
        full = np.concatenate([x[b_], ctx[b_]], 0)
        xs.append(full.T.reshape(NCH, 128, T).transpose(1, 0, 2))
    rows = [c[core * 2], c[core * 2 + 1], cc]
    cT = np.stack(rows, 0).reshape(3, NCH, 128).transpose(2, 1, 0)
    return {"xT": np.ascontiguousarray(np.stack(xs, 0)), "cT": np.ascontiguousarray(cT)}


def run(inputs, nb=2, nlayers=DEPTH, last_stage=2, ncores=8, debug_ctx=False):
    nc = build_program(nb, nlayers, last_stage, debug_ctx)
    sh = _shared_inputs(inputs)
    in_maps = []
    for core in range(ncores):
        m = dict(sh)
        m.update(_core_inputs(inputs, core, nb))
        in_maps.append(m)
    res = run_bass_kernel_spmd(nc, in_maps, core_ids=list(range(ncores)))
    outs = []
    for core in range(ncores):
        o = res.results[core]["outT"]
        ncol = o.shape[-1]
        outs.append(o.transpose(0, 3, 2, 1).reshape(nb, ncol, D))
    return outs


def kernel(**inputs):
    outs = run(inputs)
    return np.ascontiguousarray(np.concatenate(outs, 0)).astype(np.float32)
```

```python
import math
import os
import numpy as np
import concourse.bass as bass
import concourse.mybir as mybir
from concourse.bass_utils import run_bass_kernel_spmd

F32 = mybir.dt.float32
BF16 = mybir.dt.bfloat16
ALU = mybir.AluOpType
AF = mybir.ActivationFunctionType
AX = mybir.AxisListType

D = 1024
S = 2048
L = 256
T = S + L
NCH = 8
DEPTH = 4
DFF = 2816
NJ = 22
EPS = 1e-6
ALPHA = (2 * DEPTH) ** 0.25
EPS_LN = EPS / (ALPHA * ALPHA)
NEG = -30000.0
SCALE = 0.125
GC = 1.5957691216057308

NA_TILES = {0: list(range(0, 6)), 1: list(range(2, 10)), 2: list(range(6, 14)), 3: list(range(10, 16))}
NA_TID0 = {0: 0, 1: 6, 2: 6, 3: 14}
NA_NT = 20


class Res:
    __slots__ = ("name", "w", "r", "dsem", "dn")

    def __init__(self, name):
        self.name = name
        self.w = None
        self.r = {}
        self.dsem = None
        self.dn = 0


class K:
    def __init__(self, nc):
        self.nc = nc
        self.E = {"pe": nc.tensor, "act": nc.scalar, "dve": nc.vector, "pool": nc.gpsimd, "sp": nc.sync}
        self.sem = {e: nc.alloc_semaphore("s_" + e) for e in ("pe", "act", "dve")}
        self.cnt = {e: 0 for e in ("pe", "act", "dve")}
        self.seen = {e: {} for e in self.E}
        self.nwait = 0
        self.nins = 0

    def _waits(self, eng, reads, writes):
        deps = {}
        for b in reads:
            if b.w is not None:
                s, v = b.w
                if deps.get(s, 0) < v:
                    deps[s] = v
        for b in writes:
            if b.w is not None:
                s, v = b.w
                if deps.get(s, 0) < v:
                    deps[s] = v
            for s, v in b.r.items():
                if deps.get(s, 0) < v:
                    deps[s] = v
        E = self.E[eng]
        seen = self.seen[eng]
        for s, v in deps.items():
            if eng == "pe" and s is self.sem["pe"]:
                continue
            if seen.get(s, 0) < v:
                E.wait_ge(s, v)
                seen[s] = v
                self.nwait += 1

    def _record(self, ev, reads, writes):
        s, v = ev
        for b in reads:
            if b.r.get(s, 0) < v:
                b.r[s] = v
        for b in writes:
            b.w = ev
            b.r = {}

    def op(self, eng, fn, reads=(), writes=(), inc=True):
        self._waits(eng, reads, writes)
        ins = fn(self.E[eng])
        self.nins += 1
        if inc:
            self.cnt[eng] += 1
            ins.then_inc(self.sem[eng], 1)
            ev = (self.sem[eng], self.cnt[eng])
        else:
            ev = (self.sem[eng], self.cnt[eng] + 1)
        self._record(ev, reads, writes)

    def dma(self, eng, out, in_, reads=(), writes=(), dst=None, ev_override=None):
        self._waits(eng, reads, writes)
        ins = self.E[eng].dma_start(out=out, in_=in_)
        self.nins += 1
        r = dst if dst is not None else writes[0]
        if r.dsem is None:
            r.dsem = self.nc.alloc_semaphore("d_" + r.name)
        r.dn += 16
        ins.then_inc(r.dsem, 16)
        ev = ev_override if ev_override is not None else (r.dsem, r.dn)
        self._record(ev, reads, writes)
        return ev


def blocks5():
    return [(i * 512, 512) for i in range(4)] + [(S, L)]


def build_program(nb, nlayers, last_stage, debug_ctx=False):
    nc = bass.Bass("TRN2", target_bir_lowering=False)
    k = K(nc)

    def din(name, shape, dt=F32):
        return nc.dram_tensor(name, list(shape), dt, kind="ExternalInput").ap()

    xT_d = din("xT", [nb, 128, NCH, T])
    cT_d = din("cT", [128, NCH, 3])
    wada_d = din("w_ada", [DEPTH * 48, 128, NCH * 128])
    bada_d = din("b_adaT", [128, DEPTH * 48])
    lng_d = din("ln_gT", [128, DEPTH * 2 * NCH])
    lnb_d = din("ln_bT", [128, DEPTH * 2 * NCH])
    winab_d = din("w_in_ab", [2 * 24, 128, NCH * 128])
    woab_d = din("w_o_ab", [2 * 8, 128, NCH * 128])
    winc_d = din("w_in_c", [2 * 12, 128, NCH * 128])
    woc_d = din("w_o_c", [2 * 8, 128, NCH * 128])
    wup_d = din("w_up", [DEPTH * 44, 128, NCH * 128])
    wdn_d = din("w_down", [DEPTH * 8, 128, NJ * 128])
    conv_d = din("convT", [128, DEPTH * 44 * 4])
    nab_d = din("na_bias", [2 * 8 * NA_NT, 128, 512])
    cos_d = din("cosT", [128, S])
    sin_d = din("sinT", [128, S])
    perm_d = din("permT", [128, 128])
    lam_d = din("lamv", [128, 2 * 256])
    sub_d = din("sublnT", [128, 2])
    qkg_d = din("qkgT", [128, 4])
    out_d = nc.dram_tensor("outT", [nb, 128, NCH, T if debug_ctx else S], F32, kind="ExternalOutput").ap()

    def sb(name, shape, dt):
        return nc.alloc_sbuf_tensor(name, list(shape), dt)

    XT = sb("XT", [128, NCH, T], F32)
    HT = sb("HT", [128, NCH, T], BF16)
    OT = sb("OT", [128, NCH * T], BF16)
    QKV = sb("QKV", [128, 4 * T], BF16)
    QU = QKV[:, 0:T]
    KU = QKV[:, T:2 * T]
    VA = QKV[:, 2 * T:4 * T].rearrange("p (t s d) -> p t s d", t=18, s=2)
    COS = sb("COS", [128, S], F32)
    SIN = sb("SIN", [128, S], F32)
    NTMP = 6
    TMP = [sb(f"TMP{i}", [128, 512], F32) for i in range(NTMP)]
    BADA = TMP[4]
    LAMV = TMP[3]
    NPT = 2
    PT = [sb(f"PT{i}", [128, 1024], BF16) for i in range(NPT)]
    NW = 4
    WT = [sb(f"WT{i}", [128, NCH, 128], BF16) for i in range(NW)]
    NWD = 2
    WD = [QKV[:, 0:NJ * 128].rearrange("p (j o) -> p j o", j=NJ),
          QKV[:, 2 * T:2 * T + NJ * 128].rearrange("p (j o) -> p j o", j=NJ)]
    MOD = sb("MOD", [128, DEPTH * 6 * NCH * 3], F32)
    LNG = sb("LNG", [128, DEPTH * 2 * NCH], F32)
    LNB = sb("LNB", [128, DEPTH * 2 * NCH], F32)
    CONV = sb("CONV", [128, 44 * 4], F32)
    CT = sb("CT", [128, NCH, 3], F32)
    CONDT = sb("CONDT", [128, NCH, 3], F32)
    PERM = sb("PERM", [128, 128], F32)
    ONESF = sb("ONESF", [128, 128], F32)
    BLK1 = sb("BLK1", [128, 128], F32)
    ONESB = sb("ONESB", [128, 128], BF16)
    SUBG = sb("SUBG", [128, 2], F32)
    QKG = sb("QKG", [128, 4], F32)
    SM = sb("SM", [128, 64], F32)
    PSA = nc.alloc_psum_tensor("PSA", [128, 1024], F32)
    PSB = nc.alloc_psum_tensor("PSB", [128, 1024], F32)
    PS = [PSA[:, 0:512], PSA[:, 512:1024], PSB[:, 0:512], PSB[:, 512:1024]] + \
         [nc.alloc_psum_tensor(f"PS{i}", [128, 512], F32) for i in range(4, 8)]

    rXT = [Res(f"XT{c}") for c in range(NCH)]
    rHT = [Res(f"HT{c}") for c in range(NCH)]
    rOT = [Res(f"OT{c}") for c in range(NCH)]
    rQU, rKU, rVA = Res("QU"), Res("KU"), Res("VA")
    rTMP = [Res(f"TMP{i}") for i in range(NTMP)]
    rPT = [Res(f"PT{i}") for i in range(NPT)]
    rWT = [Res(f"WT{i}") for i in range(NW)]
    rCONV = Res("CONV")
    rPS = [Res(f"PS{i}") for i in range(8)]
    rMOD, rCONST, rSM = Res("MOD"), Res("CONST"), Res("SM")
    rST = Res("STORE")
    ctr = {"tmp": 0, "pt": 0, "wt": 0, "wd": 0, "g": 0, "p": 0}

    def tmp():
        i = ctr["tmp"] % NTMP
        ctr["tmp"] += 1
        return TMP[i], rTMP[i]

    def ptile():
        i = ctr["pt"] % NPT
        ctr["pt"] += 1
        return PT[i], rPT[i]

    def gbank():
        i = 4 + ctr["g"] % 4
        ctr["g"] += 1
        return PS[i], rPS[i]

    def load_w(src_ap):
        i = ctr["wt"] % NW
        ctr["wt"] += 1
        k.dma("pool", WT[i][:].rearrange("p k o -> p (k o)"), src_ap, writes=[rWT[i]])
        return WT[i], rWT[i]

    for dst, src in ((LNG, lng_d), (LNB, lnb_d), (COS, cos_d), (SIN, sin_d),
                     (PERM, perm_d), (SUBG, sub_d), (QKG, qkg_d)):
        k.dma("sp", dst[:], src, writes=[rCONST])
    k.dma("sp", BADA[:, 0:DEPTH * 48], bada_d, writes=[rTMP[4]])
    k.dma("sp", LAMV[:, 0:512], lam_d, writes=[rTMP[3]])
    k.dma("sp", CT[:].rearrange("p k r -> p (k r)"), cT_d.rearrange("p k r -> p (k r)"), writes=[rCONST])
    k.op("dve", lambda e: e.memset(ONESF[:], 1.0), writes=[rCONST])
    k.op("dve", lambda e: e.memset(ONESB[:], 1.0), writes=[rCONST])
    k.op("dve", lambda e: e.memset(BLK1[:], 0.0), writes=[rCONST])
    k.op("dve", lambda e: e.memset(BLK1[0:64, 0:64], 1.0), writes=[rCONST])
    k.op("dve", lambda e: e.memset(BLK1[64:128, 64:128], 1.0), writes=[rCONST])
    k.op("dve", lambda e: e.tensor_scalar(out=QKG[:], in0=QKG[:], scalar1=8.0, scalar2=None, op0=ALU.mult),
         reads=[rCONST], writes=[rCONST])
    k.op("act", lambda e: e.activation(out=CONDT[:].rearrange("p k r -> p (k r)"),
                                       in_=CT[:].rearrange("p k r -> p (k r)"), func=AF.Silu),
         reads=[rCONST], writes=[rCONST])

    def midx(l, m, c, r):
        return ((l * 6 + m) * NCH + c) * 3 + r

    def modap(l, m, c, r):
        i = midx(l, m, c, r)
        return MOD[:, i:i + 1]

    for l in range(nlayers):
        for og in range(3):
            ps, rps = gbank()
            for j in range(16):
                oc = og * 16 + j
                c = oc % NCH
                wa = XT[:, c, 0:1024]
                k.dma("sp", wa, wada_d[l * 48 + oc], writes=[rXT[c]])
                wav = wa.rearrange("p (k o) -> p k o", k=NCH)
                for kc in range(NCH):
                    k.op("pe", lambda e, kc=kc, wav=wav, ps=ps, j=j: e.matmul(
                        ps[:, 3 * j:3 * j + 3], wav[:, kc, :], CONDT[:, kc, :], start=(kc == 0), stop=(kc == NCH - 1)),
                        reads=[rXT[c], rCONST], writes=[rps], inc=(kc == NCH - 1))
            for j in range(16):
                oc = og * 16 + j
                i0 = (l * 48 + oc) * 3
                k.op("dve", lambda e, ps=ps, j=j, i0=i0, oc=oc: e.tensor_scalar(
                    out=MOD[:, i0:i0 + 3], in0=ps[:, 3 * j:3 * j + 3], scalar1=BADA[:, l * 48 + oc:l * 48 + oc + 1],
                    scalar2=None, op0=ALU.add), reads=[rps, rTMP[4]], writes=[rMOD])
        for m, (op_, val) in ((1, (ALU.add, 1.0)), (4, (ALU.add, 1.0)), (2, (ALU.mult, 1.0 / ALPHA)),
                              (5, (ALU.mult, 1.0 / ALPHA))):
            a = midx(l, m, 0, 0)
            k.op("dve", lambda e, a=a, op_=op_, val=val: e.tensor_scalar(
                out=MOD[:, a:a + 24], in0=MOD[:, a:a + 24], scalar1=val, scalar2=None, op0=op_),
                reads=[rMOD], writes=[rMOD])

    for i in range(2):
        l = 2 * i
        if l >= nlayers:
            continue
        lam_init = 0.8 - 0.6 * math.exp(-0.3 * l)
        t0, rt0 = tmp()
        for a in range(2):
            k.op("dve", lambda e, a=a: e.tensor_tensor(
                out=t0[:, a * 64:(a + 1) * 64], in0=LAMV[:, i * 256 + a * 128:i * 256 + a * 128 + 64],
                in1=LAMV[:, i * 256 + a * 128 + 64:i * 256 + a * 128 + 128], op=ALU.mult),
                reads=[rTMP[3]], writes=[rt0])
            k.op("dve", lambda e, a=a: e.reduce_sum(out=SM[:, i * 8 + 1 + a:i * 8 + 2 + a],
                                                    in_=t0[:, a * 64:(a + 1) * 64], axis=AX.X),
                 reads=[rt0], writes=[rSM])
        k.op("act", lambda e: e.activation(out=SM[:, i * 8 + 3:i * 8 + 5], in_=SM[:, i * 8 + 1:i * 8 + 3], func=AF.Exp),
             reads=[rSM], writes=[rSM])
        k.op("dve", lambda e: e.tensor_tensor(out=SM[:, i * 8:i * 8 + 1], in0=SM[:, i * 8 + 4:i * 8 + 5],
                                              in1=SM[:, i * 8 + 3:i * 8 + 4], op=ALU.subtract),
             reads=[rSM], writes=[rSM])
        k.op("dve", lambda e: e.tensor_scalar(out=SM[:, i * 8:i * 8 + 1], in0=SM[:, i * 8:i * 8 + 1],
                                              scalar1=-lam_init, scalar2=None, op0=ALU.add),
             reads=[rSM], writes=[rSM])
        k.op("dve", lambda e: e.tensor_scalar(out=SUBG[:, i:i + 1], in0=SUBG[:, i:i + 1], scalar1=1.0 - lam_init,
                                              scalar2=None, op0=ALU.mult), reads=[rCONST], writes=[rCONST])

    def modulate_pass(l, m_shift, m_scale, b, with_ctx):
        for c in range(NCH):
            k.op("act", lambda e, c=c: e.activation(out=HT[:, c, 0:S], in_=XT[:, c, 0:S], func=AF.Identity,
                                                    bias=modap(l, m_shift, c, b), scale=modap(l, m_scale, c, b)),
                 reads=[rXT[c], rMOD], writes=[rHT[c]])
            if with_ctx:
                k.op("act", lambda e, c=c: e.activation(out=HT[:, c, S:T], in_=XT[:, c, S:T], func=AF.Identity,
                                                        bias=modap(l, m_shift, c, 2), scale=modap(l, m_scale, c, 2)),
                     reads=[rXT[c], rMOD], writes=[rHT[c]])

    def rstd_from(ps_ap, n, scale, bias, out_t, rout, rin):
        k.op("act", lambda e: e.activation(out=out_t[:, :n], in_=ps_ap, func=AF.Ln, bias=bias, scale=scale),
             reads=rin, writes=[rout])
        k.op("act", lambda e: e.activation(out=out_t[:, :n], in_=out_t[:, :n], func=AF.Exp, scale=-0.5),
             reads=[rout], writes=[rout])

    def pbank():
        i = ctr["p"] % 8
        ctr["p"] += 1
        return PS[i], rPS[i]

    def proj_fm(w_t, rw, mode, dst_t, rdst, g8=None, half=None):
        blks = blocks5()
        st = {}

        def stage1(bi):
            c0, n = blks[bi]
            ps, rps = pbank()
            if half is None:
                for kc in range(NCH):
                    k.op("pe", lambda e, kc=kc: e.matmul(ps[:, :n], w_t[:, kc, :], HT[:, kc, c0:c0 + n],
                                                         start=(kc == 0), stop=(kc == NCH - 1)),
                         reads=[rw, rHT[kc]], writes=[rps], inc=(kc == NCH - 1))
            else:
                for hh in range(2):
                    for kc in range(NCH):
                        k.op("pe", lambda e, kc=kc, hh=hh: e.matmul(
                            ps[hh * 64:(hh + 1) * 64, :n], w_t[:, kc, half * 64:(half + 1) * 64], HT[:, kc, c0:c0 + n],
                            start=(kc == 0), stop=(kc == NCH - 1), tile_position=(0, hh * 64)),
                            reads=[rw, rHT[kc]], writes=[rps], inc=(kc == NCH - 1 and hh == 1))
            if mode == "plain" or (mode == "rope" and c0 >= S):
                k.op("act", lambda e: e.activation(out=dst_t[:, c0:c0 + n], in_=ps[:, :n], func=AF.Copy),
                     reads=[rps], writes=[rdst])
                st[bi] = None
                return
            qf, rqf = tmp()
            d = dict(qf=qf, rqf=rqf, n=n, c0=c0)
            if mode == "gqa":
                k.op("act", lambda e: e.activation(out=qf[:, :n], in_=ps[:, :n], func=AF.Identity, scale=g8),
                     reads=[rps, rCONST], writes=[rqf])
                sq, rsq = tmp()
                k.op("act", lambda e: e.activation(out=sq[:, :n], in_=ps[:, :n], func=AF.Square),
                     reads=[rps], writes=[rsq])
                d.update(sq=sq, rsq=rsq)
            else:
                k.op("act", lambda e: e.activation(out=qf[:, :n], in_=ps[:, :n], func=AF.Copy),
                     reads=[rps], writes=[rqf])
            st[bi] = d

        def stage2(bi):
            d = st[bi]
            if d is None:
                return
            n = d["n"]
            if mode == "gqa":
                ss, rss = pbank()
                k.op("pe", lambda e: e.matmul(ss[:, :n], BLK1[:], d["sq"][:, :n], start=True, stop=True),
                     reads=[d["rsq"], rCONST], writes=[rss])
                d.update(ss=ss, rss=rss)
            if d["c0"] < S:
                rot, rrot = pbank()
                k.op("pe", lambda e: e.matmul(rot[:, :n], PERM[:], d["qf"][:, :n], start=True, stop=True),
                     reads=[d["rqf"], rCONST], writes=[rrot])
                d.update(rot=rot, rrot=rrot)

        def stage3(bi):
            d = st[bi]
            if d is None:
                return
            n, c0, qf, rqf = d["n"], d["c0"], d["qf"], d["rqf"]
            dst_ap = dst_t[:, c0:c0 + n]
            if mode == "gqa":
                ss, rss = d["ss"], d["rss"]
                R, rR = tmp()
                k.op("act", lambda e: e.activation(out=R[:, :n], in_=ss[:, :n], func=AF.Ln, bias=64.0 * EPS, scale=1.0),
                     reads=[rss], writes=[rR])
                k.op("act", lambda e: e.activation(out=R[:, :n], in_=R[:, :n], func=AF.Exp, scale=-0.5),
                     reads=[rR], writes=[rR])
                d.update(R=R, rR=rR)
            if c0 < S:
                if mode == "gqa":
                    B, rB = d["sq"], d["rsq"]
                else:
                    B, rB = tmp()
                k.op("dve", lambda e: e.tensor_tensor(out=B[:, :n], in0=d["rot"][:, :n], in1=SIN[:, c0:c0 + n], op=ALU.mult),
                     reads=[d["rrot"], rCONST], writes=[rB])
                k.op("dve", lambda e: e.tensor_tensor(out=qf[:, :n], in0=qf[:, :n], in1=COS[:, c0:c0 + n], op=ALU.mult),
                     reads=[rqf, rCONST], writes=[rqf])
                if mode == "gqa":
                    k.op("dve", lambda e: e.tensor_tensor(out=qf[:, :n], in0=qf[:, :n], in1=B[:, :n], op=ALU.add),
                         reads=[rqf, rB], writes=[rqf])
                    k.op("dve", lambda e: e.tensor_tensor(out=dst_ap, in0=qf[:, :n], in1=d["R"][:, :n], op=ALU.mult),
                         reads=[rqf, d["rR"]], writes=[rdst])
                else:
                    k.op("dve", lambda e: e.tensor_tensor(out=dst_ap, in0=qf[:, :n], in1=B[:, :n], op=ALU.add),
                         reads=[rqf, rB], writes=[rdst])
            else:
                k.op("dve", lambda e: e.tensor_tensor(out=dst_ap, in0=qf[:, :n], in1=d["R"][:, :n], op=ALU.mult),
                     reads=[rqf, d["rR"]], writes=[rdst])

        nblk = len(blks)
        for it in range(nblk + 1):
            if it < nblk:
                stage1(it)
            if it >= 1:
                stage2(it - 1)
                stage3(it - 1)

    def proj_v(w_t, rw, wc0, ncols, dst_fn):
        for t0 in range(0, 18, 4):
            nt = min(4, 18 - t0)
            ps, rps = gbank()
            for j in range(nt):
                t = t0 + j
                for kc in range(NCH):
                    k.op("pe", lambda e, kc=kc, t=t, j=j: e.matmul(
                        ps[:, j * ncols:(j + 1) * ncols], HT[:, kc, t * 128:(t + 1) * 128], w_t[:, kc, wc0:wc0 + ncols],
                        start=(kc == 0), stop=(kc == NCH - 1)),
                        reads=[rw, rHT[kc]], writes=[rps], inc=(kc == NCH - 1 and j == nt - 1))
            dst_fn(ps, rps, t0, nt)

    def attend_seq(jobs, mode, bias_fns=None):
        steps = [(ji, t) for ji, jb in enumerate(jobs) for t in range(len(jb["kt"]))]
        st = {}
        pts = {}
        deferred = []

        def job_state(ji):
            if ji not in st:
                d = dict(accs=[gbank() for _ in range(2 if mode == "aug" else 4)])
                if mode == "diff":
                    d["dacc"] = tmp()
                st[ji] = d
            return st[ji]

        for g in range(len(steps) + 1):
            if g < len(steps):
                ji, t = steps[g]
                jb = jobs[ji]
                c0, nq = jb["c0"], jb["nq"]
                tok0, vaps, tid = jb["kt"][t]
                p = g % 2
                SP = PSA if p == 0 else PSB
                rS = [rPS[2 * p], rPS[2 * p + 1]]
                for s_ in range(2):
                    k.op("pe", lambda e, tok0=tok0, SP=SP, s_=s_, c0=c0, nq=nq: e.matmul(
                        SP[:, s_ * 512:s_ * 512 + nq], KU[64 * s_:64 * s_ + 64, tok0:tok0 + 128],
                        QU[64 * s_:64 * s_ + 64, c0:c0 + nq], start=True, stop=True),
                        reads=[rKU, rQU], writes=[rS[s_]], inc=(s_ == 1))
                P, rP = ptile()
                if bias_fns is not None and tid is not None:
                    for s_ in range(2):
                        bt, rbt = tmp()
                        k.dma("sp", bt[:, :nq], bias_fns[s_](tid), writes=[rbt])
                        k.op("dve", lambda e, SP=SP, bt=bt, s_=s_, nq=nq: e.scalar_tensor_tensor(
                            out=bt[:, :nq], in0=SP[:, s_ * 512:s_ * 512 + nq], scalar=SCALE, in1=bt[:, :nq],
                            op0=ALU.mult, op1=ALU.add), reads=[rS[s_], rbt], writes=[rbt])
                        k.op("act", lambda e, bt=bt, P=P, s_=s_, nq=nq: e.activation(
                            out=P[:, s_ * 512:s_ * 512 + nq], in_=bt[:, :nq], func=AF.Exp),
                            reads=[rbt], writes=[rP])
                else:
                    if nq == 512:
                        src, dst = SP[:, 0:1024], P[:, 0:1024]
                    else:
                        src = SP[:, 0:1024].rearrange("p (s n) -> p s n", s=2)[:, :, 0:nq]
                        dst = P[:, 0:1024].rearrange("p (s n) -> p s n", s=2)[:, :, 0:nq]
                    k.op("act", lambda e, src=src, dst=dst: e.activation(out=dst, in_=src, func=AF.Exp, scale=SCALE),
                         reads=rS, writes=[rP])
                pts[g] = (P, rP)
            if g >= 1:
                ji, tt = steps[g - 1]
                jb = jobs[ji]
                c0, nq = jb["c0"], jb["nq"]
                nt = len(jb["kt"])
                tok0, vaps, tid = jb["kt"][tt]
                P, rP = pts.pop(g - 1)
                last = (tt == nt - 1)
                d = job_state(ji)
                accs = d["accs"]
                Os = [accs[0], accs[1]] if mode == "aug" else [accs[0], accs[2]]
                for s_ in range(2):
                    O, rO = Os[s_]
                    k.op("pe", lambda e, vap=vaps[s_], P=P, O=O, s_=s_, nq=nq, tt=tt, last=last: e.matmul(
                        O[:, :nq], vap, P[:, s_ * 512:s_ * 512 + nq], start=(tt == 0), stop=last),
                        reads=[rVA, rP], writes=[rO], inc=last)
                if mode == "diff":
                    A, rA = d["dacc"]
                    if tt == 0:
                        k.op("dve", lambda e, P=P, A=A, nq=nq: e.tensor_copy(out=A[:, :nq], in_=P[:, 0:nq]),
                             reads=[rP], writes=[rA])
                    else:
                        k.op("dve", lambda e, P=P, A=A, nq=nq: e.tensor_tensor(
                            out=A[:, :nq], in0=A[:, :nq], in1=P[:, 0:nq], op=ALU.add), reads=[rP, rA], writes=[rA])
                    D1, rD1 = accs[3]
                    k.op("pe", lambda e, P=P, D1=D1, nq=nq, tt=tt, last=last: e.matmul(
                        D1[:, :nq], ONESB[:], P[:, 512:512 + nq], start=(tt == 0), stop=last),
                        reads=[rCONST, rP], writes=[rD1], inc=last)
                    if last:
                        D0, rD0 = accs[1]
                        k.op("pe", lambda e, A=A, D0=D0, nq=nq: e.matmul(D0[:, :nq], ONESF[:], A[:, :nq],
                                                                         start=True, stop=True),
                             reads=[rCONST, rA], writes=[rD0])
                if last:
                    later = jb["fin"](accs, nq, c0)
                    if later is not None:
                        deferred.append((g + later[0], later[1]))
            while deferred and (deferred[0][0] <= g or g == len(steps)):
                deferred.pop(0)[1]()

    def recip_act(dst_ap, src_ap, rsrc, rdst):
        k.op("act", lambda e: e.activation(out=dst_ap, in_=src_ap, func=AF.Ln), reads=rsrc, writes=[rdst])
        k.op("act", lambda e: e.activation(out=dst_ap, in_=dst_ap, func=AF.Exp, scale=-1.0), reads=[rdst], writes=[rdst])

    def aug_fin(ot_chunk, act_recip):
        def fin(accs, nq, c0):
            base = ot_chunk * T + c0
            rds = []
            for s_ in range(2):
                O, rO = accs[s_]
                rd, rrd = tmp()
                if act_recip:
                    recip_act(rd[64:128, :nq], O[64:128, :nq], [rO], rrd)
                else:
                    k.op("dve", lambda e, O=O, rd=rd: e.reciprocal(out=rd[64:128, :nq], in_=O[64:128, :nq]),
                         reads=[rO], writes=[rrd])
                rds.append((rd, rrd))

            def mults():
                for s_ in range(2):
                    O, rO = accs[s_]
                    rd, rrd = rds[s_]
                    k.op("dve", lambda e, O=O, rd=rd, s_=s_: e.tensor_tensor(
                        out=OT[64 * s_:64 * s_ + 64, base:base + nq], in0=O[0:64, :nq], in1=rd[64:128, :nq],
                        op=ALU.mult), reads=[rO, rrd], writes=[rOT[ot_chunk]])
            if act_recip:
                return (2, mults)
            mults()
            return None
        return fin

    def ctx_tiles(vfn0, vfn1):
        return [(S + j * 128, (vfn0(16 + j), vfn1(16 + j)), None) for j in range(2)]

    def res_ln_block(l, which, r, c0, n, ybank_fn, st1, rst1, st2, rst2):
        mg = 2 if which == 0 else 5
        for oc in range(NCH):
            yp, ryp = ybank_fn(oc)
            xs = XT[:, oc, c0:c0 + n]
            k.op("dve", lambda e, yp=yp, xs=xs, oc=oc: e.scalar_tensor_tensor(
                out=xs, in0=yp, scalar=modap(l, mg, oc, r), in1=xs, op0=ALU.mult, op1=ALU.add),
                reads=[ryp, rXT[oc], rMOD], writes=[rXT[oc]])
            sq, rsq = tmp()
            k.op("act", lambda e, xs=xs, sq=sq: e.activation(out=sq[:, :n], in_=xs, func=AF.Square),
                 reads=[rXT[oc]], writes=[rsq])
            k.op("pe", lambda e, xs=xs, oc=oc: e.matmul(st1[:, :n], ONESF[:], xs, start=(oc == 0), stop=(oc == NCH - 1)),
                 reads=[rXT[oc], rCONST], writes=[rst1], inc=False)
            k.op("pe", lambda e, sq=sq, oc=oc: e.matmul(st2[:, :n], ONESF[:], sq[:, :n], start=(oc == 0),
                                                        stop=(oc == NCH - 1)),
                 reads=[rsq, rCONST], writes=[rst2])
        mean, rmean = tmp()
        k.op("dve", lambda e: e.tensor_scalar(out=mean[:, :n], in0=st1[:, :n], scalar1=1.0 / D, scalar2=None,
                                              op0=ALU.mult), reads=[rst1], writes=[rmean])
        var, rvar = tmp()
        k.op("dve", lambda e: e.tensor_tensor(out=var[:, :n], in0=mean[:, :n], in1=mean[:, :n], op=ALU.mult),
             reads=[rmean], writes=[rvar])
        k.op("dve", lambda e: e.scalar_tensor_tensor(out=var[:, :n], in0=st2[:, :n], scalar=1.0 / D, in1=var[:, :n],
                                                     op0=ALU.mult, op1=ALU.subtract),
             reads=[rst2, rvar], writes=[rvar])
        rstd_from(var[:, :n], n, 1.0, EPS_LN, var, rvar, [rvar])
        k.op("dve", lambda e: e.scalar_tensor_tensor(out=mean[:, :n], in0=mean[:, :n], scalar=-1.0, in1=var[:, :n],
                                                     op0=ALU.mult, op1=ALU.mult),
             reads=[rmean, rvar], writes=[rmean])
        gi = (l * 2 + which) * NCH
        for oc in range(NCH):
            xs = XT[:, oc, c0:c0 + n]
            k.op("dve", lambda e, xs=xs: e.tensor_tensor(out=xs, in0=xs, in1=var[:, :n], op=ALU.mult),
                 reads=[rXT[oc], rvar], writes=[rXT[oc]])
            k.op("dve", lambda e, xs=xs: e.tensor_tensor(out=xs, in0=xs, in1=mean[:, :n], op=ALU.add),
                 reads=[rXT[oc], rmean], writes=[rXT[oc]])
            k.op("act", lambda e, xs=xs, oc=oc: e.activation(out=xs, in_=xs, func=AF.Identity,
                                                             bias=LNB[:, gi + oc:gi + oc + 1],
                                                             scale=LNG[:, gi + oc:gi + oc + 1]),
                 reads=[rXT[oc], rCONST], writes=[rXT[oc]])

    def mixer_even(l, b, need_ctx):
        i = l // 2
        k.op("dve", lambda e: e.memset(VA[:, :, :, 64:128], 1.0), writes=[rVA])
        qblocks = [(qb * 512, 512) for qb in range(4)] + ([(S, L)] if need_ctx else [])
        for u in range(4):
            wq, rwq = load_w(winab_d[i * 24 + u])
            proj_fm(wq, rwq, "plain", QU, rQU)
            wk, rwk = load_w(winab_d[i * 24 + 4 + u])
            proj_fm(wk, rwk, "plain", KU, rKU)
            wv, rwv = load_w(winab_d[i * 24 + 8 + u])

            def vdst(ps, rps, t0, nt):
                k.op("act", lambda e: e.activation(
                    out=VA[:, t0:t0 + nt, :, 0:64],
                    in_=ps[:, 0:nt * 128].rearrange("p (t s d) -> p t s d", t=nt, s=2), func=AF.Copy),
                    reads=[rps], writes=[rVA])
            proj_v(wv, rwv, 0, 128, vdst)
            vfn0 = lambda t: VA[:, t, 0, :]
            vfn1 = lambda t: VA[:, t, 1, :]
            bias_fns = tuple((lambda tid, h=2 * u + s_: nab_d[(i * 8 + h) * NA_NT + tid]) for s_ in range(2))
            jobs = []
            for qi, (c0, nq) in enumerate(qblocks):
                if c0 < S:
                    kt = [(tl * 128, (vfn0(tl), vfn1(tl)), NA_TID0[qi] + j) for j, tl in enumerate(NA_TILES[qi])]
                    kt += ctx_tiles(vfn0, vfn1)
                else:
                    kt = ctx_tiles(vfn0, vfn1)
                jobs.append(dict(c0=c0, nq=nq, kt=kt, fin=aug_fin(u, True)))
            attend_seq(jobs, "aug", bias_fns)
        for u in range(4):
            wq, rwq = load_w(winab_d[i * 24 + 12 + u])
            proj_fm(wq, rwq, "rope", QU, rQU)
            wk, rwk = load_w(winab_d[i * 24 + 16 + u])
            proj_fm(wk, rwk, "rope", KU, rKU)
            wv, rwv = load_w(winab_d[i * 24 + 20 + u])

            def vdst(ps, rps, t0, nt):
                k.op("act", lambda e: e.activation(
                    out=VA[:, t0:t0 + nt, 1, :], in_=ps[:, 0:nt * 128].rearrange("p (t d) -> p t d", t=nt),
                    func=AF.Copy), reads=[rps], writes=[rVA])
            proj_v(wv, rwv, 0, 128, vdst)
            vfn = lambda t: VA[:, t, 1, :]
            def diff_fin(accs, nq, c0, u=u):
                (O0, rO0), (D0, rD0), (O1, rO1), (D1, rD1) = accs
                r1, rr1 = tmp()
                recip_act(r1[:, :nq], D0[:, :nq], [rD0], rr1)
                r2, rr2 = tmp()
                recip_act(r2[:, :nq], D1[:, :nq], [rD1], rr2)
                a1, ra1 = r1, rr1
                k.op("dve", lambda e: e.tensor_tensor(out=a1[:, :nq], in0=O0[:, :nq], in1=r1[:, :nq], op=ALU.mult),
                     reads=[rO0, rr1], writes=[ra1])
                k.op("dve", lambda e: e.tensor_tensor(out=r2[:, :nq], in0=O1[:, :nq], in1=r2[:, :nq], op=ALU.mult),
                     reads=[rO1, rr2], writes=[rr2])
                k.op("dve", lambda e: e.scalar_tensor_tensor(out=a1[:, :nq], in0=r2[:, :nq], scalar=SM[:, i * 8:i * 8 + 1],
                                                             in1=a1[:, :nq], op0=ALU.mult, op1=ALU.add),
                     reads=[rr2, ra1, rSM], writes=[ra1])
                sq, rsq = tmp()
                k.op("act", lambda e: e.activation(out=sq[:, :nq], in_=a1[:, :nq], func=AF.Square),
                     reads=[ra1], writes=[rsq])
                ss, rss = gbank()
                k.op("pe", lambda e: e.matmul(ss[:, :nq], ONESF[:], sq[:, :nq], start=True, stop=True),
                     reads=[rsq, rCONST], writes=[rss])
                rstd_from(ss[:, :nq], nq, 1.0 / 128.0, EPS, sq, rsq, [rss])
                base = (4 + u) * T + c0
                k.op("dve", lambda e: e.scalar_tensor_tensor(out=OT[:, base:base + nq], in0=a1[:, :nq],
                                                             scalar=SUBG[:, i:i + 1], in1=sq[:, :nq],
                                                             op0=ALU.mult, op1=ALU.mult),
                     reads=[ra1, rsq, rCONST], writes=[rOT[4 + u]])
            jobs = []
            for (c0, nq) in qblocks:
                if c0 < S:
                    kt = [(tl * 128, (vfn(tl), vfn(tl)), None) for tl in range(18)]
                else:
                    kt = ctx_tiles(vfn, vfn)
                jobs.append(dict(c0=c0, nq=nq, kt=kt, fin=diff_fin))
            attend_seq(jobs, "diff")
        return woab_d, i * 8

    def mixer_odd(l, b, need_ctx):
        i = l // 2
        k.op("dve", lambda e: e.memset(VA[:, :, 0, 64:128], 1.0), writes=[rVA])
        qblocks = [(qb * 512, 512) for qb in range(4)] + ([(S, L)] if need_ctx else [])
        for u in range(8):
            g = u // 2
            wq, rwq = load_w(winc_d[i * 12 + u])
            proj_fm(wq, rwq, "gqa", QU, rQU, g8=QKG[:, 2 * i:2 * i + 1])
            if u % 2 == 0:
                wk, rwk = load_w(winc_d[i * 12 + 8 + g // 2])
                proj_fm(wk, rwk, "gqa", KU, rKU, g8=QKG[:, 2 * i + 1:2 * i + 2], half=g % 2)
                wv, rwv = load_w(winc_d[i * 12 + 10 + g // 2])

                def vdst(ps, rps, t0, nt):
                    k.op("act", lambda e: e.activation(
                        out=VA[:, t0:t0 + nt, 0, 0:64], in_=ps[:, 0:nt * 64].rearrange("p (t d) -> p t d", t=nt),
                        func=AF.Copy), reads=[rps], writes=[rVA])
                proj_v(wv, rwv, (g % 2) * 64, 64, vdst)
            vfn = lambda t: VA[:, t, 0, :]
            jobs = []
            for (c0, nq) in qblocks:
                if c0 < S:
                    kt = [(tl * 128, (vfn(tl), vfn(tl)), None) for tl in range(18)]
                else:
                    kt = ctx_tiles(vfn, vfn)
                jobs.append(dict(c0=c0, nq=nq, kt=kt, fin=aug_fin(u, False)))
            attend_seq(jobs, "aug")
        return woc_d, i * 8

    def out_proj_ln(l, b, need_ctx, wo_d, wo_base):
        blks = blocks5() if need_ctx else blocks5()[:4]
        bigs = [blks[0:2], blks[2:4]] + ([blks[4:5]] if need_ctx else [])
        for big in bigs:
            for oc in range(NCH):
                w, rw = load_w(wo_d[wo_base + oc])
                for si, (c0, n) in enumerate(big):
                    r = b if c0 < S else 2
                    st1, rst1, st2, rst2 = PS[2 * si], rPS[2 * si], PS[2 * si + 1], rPS[2 * si + 1]
                    ps, rps = gbank()
                    for kc in range(NCH):
                        k.op("pe", lambda e, kc=kc: e.matmul(ps[:, :n], w[:, kc, :], OT[:, kc * T + c0:kc * T + c0 + n],
                                                             start=(kc == 0), stop=(kc == NCH - 1)),
                             reads=[rw, rOT[kc]], writes=[rps], inc=(kc == NCH - 1))
                    xs = XT[:, oc, c0:c0 + n]
                    k.op("dve", lambda e, xs=xs, r=r: e.scalar_tensor_tensor(
                        out=xs, in0=ps[:, :n], scalar=modap(l, 2, oc, r), in1=xs, op0=ALU.mult, op1=ALU.add),
                        reads=[rps, rXT[oc], rMOD], writes=[rXT[oc]])
                    sq, rsq = tmp()
                    k.op("act", lambda e, xs=xs, sq=sq: e.activation(out=sq[:, :n], in_=xs, func=AF.Square),
                         reads=[rXT[oc]], writes=[rsq])
                    k.op("pe", lambda e, xs=xs: e.matmul(st1[:, :n], ONESF[:], xs, start=(oc == 0), stop=(oc == NCH - 1)),
                         reads=[rXT[oc], rCONST], writes=[rst1], inc=False)
                    k.op("pe", lambda e, sq=sq: e.matmul(st2[:, :n], ONESF[:], sq[:, :n], start=(oc == 0),
                                                         stop=(oc == NCH - 1)),
                         reads=[rsq, rCONST], writes=[rst2])
            for si, (c0, n) in enumerate(big):
                ln_finish(l, 0, c0, n, PS[2 * si], rPS[2 * si], PS[2 * si + 1], rPS[2 * si + 1])

    def ffn(l, b, need_ctx):
        subs = [(0, 410, 0, S), (410, 410, 0, S), (820, 410, 0, S), (1230, 410, 0, S), (1640, 408, 0, S)]
        bigs = [[subs[0], subs[1]], [subs[2], subs[3]], [subs[4]]]
        if need_ctx:
            bigs[2].append((S, L, S, T))
        rACT = rOT
        k.dma("sp", CONV[:], conv_d[:, l * 176:(l + 1) * 176], writes=[rCONV])
        pending = [None]
        for big in bigs:
            for j in range(NJ):
                wa, rwa = load_w(wup_d[l * 44 + j])
                wg, rwg = load_w(wup_d[l * 44 + NJ + j])
                for si, (s0, n, q0, q1) in enumerate(big):
                    lo = 1 if s0 - 1 < q0 else 0
                    hi = n - 1 if s0 + n + 1 > q1 else n
                    ra0 = s0 - 1 + lo
                    ncol = (hi + 2) - lo
                    tts = []
                    for which, (w, rw) in enumerate(((wa, rwa), (wg, rwg))):
                        bi = (j % 2) * 4 + si * 2 + which
                        ps, rps = PS[bi], rPS[bi]
                        for kc in range(NCH):
                            k.op("pe", lambda e, kc=kc, w=w, ps=ps: e.matmul(
                                ps[:, lo:lo + ncol], w[:, kc, :], HT[:, kc, ra0:ra0 + ncol],
                                start=(kc == 0), stop=(kc == NCH - 1)),
                                reads=[rw, rHT[kc]], writes=[rps], inc=(kc == NCH - 1))
                        ci = (which * NJ + j) * 4
                        tt, rtt = tmp()
                        k.op("act", lambda e, ps=ps, tt=tt, ci=ci: e.activation(
                            out=tt[:, :n], in_=ps[:, 1:n + 1], func=AF.Identity, bias=CONV[:, ci + 3:ci + 4],
                            scale=CONV[:, ci + 1:ci + 2]), reads=[rps, rCONV], writes=[rtt])
                        k.op("dve", lambda e, ps=ps, tt=tt, ci=ci: e.scalar_tensor_tensor(
                            out=tt[:, lo:n], in0=ps[:, lo:n], scalar=CONV[:, ci:ci + 1], in1=tt[:, lo:n],
                            op0=ALU.mult, op1=ALU.add), reads=[rps, rtt, rCONV], writes=[rtt])
                        k.op("dve", lambda e, ps=ps, tt=tt, ci=ci: e.scalar_tensor_tensor(
                            out=tt[:, 0:hi], in0=ps[:, 2:hi + 2], scalar=CONV[:, ci + 2:ci + 3], in1=tt[:, 0:hi],
                            op0=ALU.mult, op1=ALU.add), reads=[rps, rtt, rCONV], writes=[rtt])
                        tts.append((tt, rtt))
                    (ta, rta), (tg, rtg) = tts
                    ab = (j * 2 + si) * 410

                    def fin(ta=ta, rta=rta, tg=tg, rtg=rtg, n=n, ab=ab, j=j):
                        k.op("act", lambda e: e.activation(out=tg[:, :n], in_=tg[:, :n], func=AF.Gelu_apprx_tanh),
                             reads=[rtg], writes=[rtg])
                        k.op("dve", lambda e: e.tensor_tensor(out=OT[:, ab:ab + n], in0=ta[:, :n], in1=tg[:, :n],
                                                              op=ALU.mult),
                             reads=[rta, rtg], writes=[rACT[j % NCH]])
                    if pending[0] is not None:
                        pending[0]()
                    pending[0] = fin
            if pending[0] is not None:
                pending[0]()
                pending[0] = None
            ybanks = {}
            for oc in range(NCH):
                iwd = ctr["wd"] % NWD
                ctr["wd"] += 1
                wdres = [rQU, rKU] if iwd == 0 else [rVA]
                wdsem = rQU if iwd == 0 else rVA
                k.dma("pool", (QKV[:, 0:NJ * 128] if iwd == 0 else QKV[:, 2 * T:2 * T + NJ * 128]), wdn_d[l * 8 + oc],
                      writes=wdres, dst=wdsem)
                for si, (s0, n, q0, q1) in enumerate(big):
                    bi = (oc % 2) * 2 + si
                    ps, rps = PS[bi], rPS[bi]
                    for j in range(NJ):
                        ab = (j * 2 + si) * 410
                        k.op("pe", lambda e, j=j, ab=ab, ps=ps, iwd=iwd: e.matmul(
                            ps[:, :n], WD[iwd][:, j, :], OT[:, ab:ab + n], start=(j == 0), stop=(j == NJ - 1)),
                            reads=wdres + [rACT[j % NCH]], writes=[rps], inc=(j == NJ - 1))
                    ybanks[(oc, si)] = (ps, rps)
                    r = b if s0 < S else 2
                    st1, rst1, st2, rst2 = PS[4 + 2 * si], rPS[4 + 2 * si], PS[5 + 2 * si], rPS[5 + 2 * si]
                    xs = XT[:, oc, s0:s0 + n]
                    k.op("dve", lambda e, ps=ps, xs=xs, oc=oc, r=r: e.scalar_tensor_tensor(
                        out=xs, in0=ps[:, :n], scalar=modap(l, 5, oc, r), in1=xs, op0=ALU.mult, op1=ALU.add),
                        reads=[rps, rXT[oc], rMOD], writes=[rXT[oc]])
                    sq, rsq = tmp()
                    k.op("act", lambda e, xs=xs, sq=sq: e.activation(out=sq[:, :n], in_=xs, func=AF.Square),
                         reads=[rXT[oc]], writes=[rsq])
                    k.op("pe", lambda e, xs=xs, oc=oc, st1=st1: e.matmul(st1[:, :n], ONESF[:], xs, start=(oc == 0),
                                                                         stop=(oc == NCH - 1)),
                         reads=[rXT[oc], rCONST], writes=[rst1], inc=False)
                    k.op("pe", lambda e, sq=sq, oc=oc, st2=st2: e.matmul(st2[:, :n], ONESF[:], sq[:, :n], start=(oc == 0),
                                                                         stop=(oc == NCH - 1)),
                         reads=[rsq, rCONST], writes=[rst2])
            for si, (s0, n, q0, q1) in enumerate(big):
                st1, rst1, st2, rst2 = PS[4 + 2 * si], rPS[4 + 2 * si], PS[5 + 2 * si], rPS[5 + 2 * si]
                ln_finish(l, 1, s0, n, st1, rst1, st2, rst2)

    def ln_finish(l, which, c0, n, st1, rst1, st2, rst2):
        mean, rmean = tmp()
        k.op("dve", lambda e: e.tensor_scalar(out=mean[:, :n], in0=st1[:, :n], scalar1=1.0 / D, scalar2=None,
                                              op0=ALU.mult), reads=[rst1], writes=[rmean])
        var, rvar = tmp()
        k.op("dve", lambda e: e.tensor_tensor(out=var[:, :n], in0=mean[:, :n], in1=mean[:, :n], op=ALU.mult),
             reads=[rmean], writes=[rvar])
        k.op("dve", lambda e: e.scalar_tensor_tensor(out=var[:, :n], in0=st2[:, :n], scalar=1.0 / D, in1=var[:, :n],
                                                     op0=ALU.mult, op1=ALU.subtract),
             reads=[rst2, rvar], writes=[rvar])
        rstd_from(var[:, :n], n, 1.0, EPS_LN, var, rvar, [rvar])
        k.op("dve", lambda e: e.scalar_tensor_tensor(out=mean[:, :n], in0=mean[:, :n], scalar=-1.0, in1=var[:, :n],
                                                     op0=ALU.mult, op1=ALU.mult),
             reads=[rmean, rvar], writes=[rmean])
        gi = (l * 2 + which) * NCH
        for oc in range(NCH):
            xs = XT[:, oc, c0:c0 + n]
            k.op("dve", lambda e, xs=xs: e.tensor_tensor(out=xs, in0=xs, in1=var[:, :n], op=ALU.mult),
                 reads=[rXT[oc], rvar], writes=[rXT[oc]])
            k.op("dve", lambda e, xs=xs: e.tensor_tensor(out=xs, in0=xs, in1=mean[:, :n], op=ALU.add),
                 reads=[rXT[oc], rmean], writes=[rXT[oc]])
            k.op("act", lambda e, xs=xs, oc=oc: e.activation(out=xs, in_=xs, func=AF.Identity,
                                                             bias=LNB[:, gi + oc:gi + oc + 1],
                                                             scale=LNG[:, gi + oc:gi + oc + 1]),
                 reads=[rXT[oc], rCONST], writes=[rXT[oc]])

    nstore = 0
    for b in range(nb):
        for c in range(NCH):
            k.dma("sp", XT[:, c, :], xT_d[b, :, c, :], writes=[rXT[c]])
        for l in range(nlayers):
            need_ctx = l < DEPTH - 1
            modulate_pass(l, 0, 1, b, True)
            if l % 2 == 0:
                wo_d, wo_base = mixer_even(l, b, need_ctx)
            else:
                wo_d, wo_base = mixer_odd(l, b, need_ctx)
            out_proj_ln(l, b, need_ctx, wo_d, wo_base)
            if l == nlayers - 1 and last_stage == 1:
                break
            modulate_pass(l, 3, 4, b, need_ctx)
            ffn(l, b, need_ctx)
        nstore += NCH
        ncol_out = T if debug_ctx else S
        for c in range(NCH):
            k.dma("sp", out_d[b, :, c, :], XT[:, c, 0:ncol_out], reads=[rXT[c]], dst=rST,
                  ev_override=None)
        for c in range(NCH):
            rXT[c].r[rST.dsem] = rST.dn
    nc.sync.wait_ge(rST.dsem, rST.dn)
    print(f"[kernel] instructions={k.nins} waits={k.nwait} cnt={k.cnt}", flush=True)
    return nc


def _tile_w(w, ncol_tiles):
    kin = w.shape[0] // 128
    a = w.reshape(kin, 128, ncol_tiles, 128).transpose(2, 1, 0, 3)
    return np.ascontiguousarray(a).reshape(ncol_tiles, 128, kin * 128)


def _fm(v):
    sh = v.shape
    n = sh[-1] // 128
    a = v.reshape(sh[:-1] + (n, 128))
    a = np.moveaxis(a, -1, 0)
    return np.ascontiguousarray(a).reshape(128, -1)


def _na_bias(rpb):
    H = rpb.shape[0]
    c = np.arange(64)
    cs = np.clip(c - 8, 0, 48)
    col_ok = (c[None, :] >= cs[:, None]) & (c[None, :] < cs[:, None] + 16)
    dc = np.clip(c[None, :] - c[:, None] + 15, 0, 30)
    bc = np.where(col_ok[None, None], rpb[:, :, dc], np.float32(NEG)).astype(np.float32)
    bcT = np.ascontiguousarray(bc.transpose(0, 1, 3, 2))
    out = np.full((H, NA_NT, 128, 512), NEG, np.float32)
    for qb in (0, 1, 3):
        for j, tl in enumerate(NA_TILES[qb]):
            tid = NA_TID0[qb] + j
            for ii in range(2):
                kr = 2 * tl + ii
                for jj in range(8):
                    qr = 8 * qb + jj
                    rs = min(max(qr - 4, 0), 24)
                    if rs <= kr < rs + 8:
                        out[:, tid, ii * 64:(ii + 1) * 64, jj * 64:(jj + 1) * 64] = bcT[:, kr - qr + 7]
    return out


def _rope_tables():
    inv = (10000.0 ** (-np.arange(16, dtype=np.float32) / 16)).astype(np.float32)
    t = np.arange(S)
    row = (t // 64).astype(np.float32)
    col = (t % 64).astype(np.float32)
    ar = row[:, None] * inv
    ac = col[:, None] * inv
    ang = np.concatenate([ar, ar, ac, ac], -1).astype(np.float32)
    cos = np.cos(ang).astype(np.float32).T
    sin = np.sin(ang).astype(np.float32).T
    sign = np.ones(64, np.float32)
    sign[0:16] = -1
    sign[32:48] = -1
    sinS = sin * sign[:, None]
    perm = np.zeros((128, 128), np.float32)
    for blk in range(2):
        for d in range(64):
            src = d + 16 if (d < 16 or 32 <= d < 48) else d - 16
            perm[blk * 64 + src, blk * 64 + d] = 1.0
    return (np.ascontiguousarray(np.tile(cos, (2, 1))), np.ascontiguousarray(np.tile(sinS, (2, 1))), perm)


def _shared_inputs(inp):
    f = np.float32
    sh = {}
    sh["w_ada"] = np.concatenate([_tile_w(np.asarray(inp["w_ada"][l], f), 48) for l in range(DEPTH)], 0)
    sh["b_adaT"] = _fm(np.asarray(inp["b_ada"], f).reshape(-1))
    sh["ln_gT"] = _fm(np.asarray(inp["ln_g"], f).reshape(-1))
    sh["ln_bT"] = _fm(np.asarray(inp["ln_b"], f).reshape(-1))
    sh["w_in_ab"] = np.concatenate([_tile_w(np.asarray(inp["w_in_ab"][i], f), 24) for i in range(2)], 0)
    sh["w_o_ab"] = np.concatenate([_tile_w(np.asarray(inp["w_o_ab"][i], f), 8) for i in range(2)], 0)
    sh["w_in_c"] = np.concatenate([_tile_w(np.asarray(inp["w_in_c"][i], f), 12) for i in range(2)], 0)
    sh["w_o_c"] = np.concatenate([_tile_w(np.asarray(inp["w_o_c"][i], f), 8) for i in range(2)], 0)
    sh["w_up"] = np.concatenate([_tile_w(np.asarray(inp["w_up"][l], f), 44) for l in range(DEPTH)], 0)
    sh["w_down"] = np.concatenate([_tile_w(np.asarray(inp["w_down"][l], f), 8) for l in range(DEPTH)], 0)
    cw = np.asarray(inp["conv_w"], f)
    cb = np.asarray(inp["conv_b"], f)
    cv = np.concatenate([cw, cb[:, None, :]], 1)
    cv = cv.reshape(DEPTH, 4, 44, 128).transpose(3, 0, 2, 1)
    sh["convT"] = np.ascontiguousarray(cv).reshape(128, -1)
    sh["na_bias"] = np.concatenate([_na_bias(np.asarray(inp["na_rpb"][i], f)) for i in range(2)], 0).reshape(
        2 * 8 * NA_NT, 128, 512)
    cosT, sinT, perm = _rope_tables()
    sh["cosT"], sh["sinT"], sh["permT"] = cosT, sinT, perm
    lam = np.asarray(inp["diff_lambda"], f).reshape(1, 2 * 256)
    sh["lamv"] = np.ascontiguousarray(np.broadcast_to(lam, (128, 512)))
    sh["sublnT"] = np.ascontiguousarray(np.asarray(inp["diff_subln"], f).T)
    g = np.asarray(inp["gqa_qk_norm"], f).reshape(4, 64)
    sh["qkgT"] = np.ascontiguousarray(np.tile(g.T, (2, 1)))
    return sh


def _core_inputs(inp, core, nb):
    f = np.float32
    x = np.asarray(inp["x"], f)
    ctx = np.asarray(inp["ctx"], f)
    c = np.asarray(inp["c"], f)
    cc = np.asarray(inp["c_ctx"], f)
    bs = [core * 2 + j for j in range(nb)]
    xs = []
    for b_ in bs:
        full = np.concatenate([x[b_], ctx[b_]], 0)
        xs.append(full.T.reshape(NCH, 128, T).transpose(1, 0, 2))
    rows = [c[core * 2], c[core * 2 + 1], cc]
    cT = np.stack(rows, 0).reshape(3, NCH, 128).transpose(2, 1, 0)
    return {"xT": np.ascontiguousarray(np.stack(xs, 0)), "cT": np.ascontiguousarray(cT)}


def run(inputs, nb=2, nlayers=DEPTH, last_stage=2, ncores=8, debug_ctx=False):
    nc = build_program(nb, nlayers, last_stage, debug_ctx)
    sh = _shared_inputs(inputs)
    in_maps = []
    for core in range(ncores):
        m = dict(sh)
        m.update(_core_inputs(inputs, core, nb))
        in_maps.append(m)
    res = run_bass_kernel_spmd(nc, in_maps, core_ids=list(range(ncores)))
    outs = []
    for core in range(ncores):
        o = res.results[core]["outT"]
        ncol = o.shape[-1]
        outs.append(o.transpose(0, 3, 2, 1).reshape(nb, ncol, D))
    return outs


def kernel(**inputs):
    outs = run(inputs)
    return np.ascontiguousarray(np.concatenate(outs, 0)).astype(np.float32)
```

```python
import math
import os
import numpy as np
import concourse.bass as bass
import concourse.mybir as mybir
from concourse.bass_utils import run_bass_kernel_spmd

F32 = mybir.dt.float32
BF16 = mybir.dt.bfloat16
ALU = mybir.AluOpType
AF = mybir.ActivationFunctionType
AX = mybir.AxisListType

D = 1024
S = 2048
L = 256
T = S + L
NCH = 8
DEPTH = 4
DFF = 2816
NJ = 22
EPS = 1e-6
ALPHA = (2 * DEPTH) ** 0.25
EPS_LN = EPS / (ALPHA * ALPHA)
NEG = -30000.0
SCALE = 0.125
GC = 1.5957691216057308

NA_TILES = {0: list(range(0, 6)), 1: list(range(2, 10)), 2: list(range(6, 14)), 3: list(range(10, 16))}
NA_TID0 = {0: 0, 1: 6, 2: 6, 3: 14}
NA_NT = 20


class Res:
    __slots__ = ("name", "w", "r", "dsem", "dn")

    def __init__(self, name):
        self.name = name
        self.w = None
        self.r = {}
        self.dsem = None
        self.dn = 0


class K:
    def __init__(self, nc):
        self.nc = nc
        self.E = {"pe": nc.tensor, "act": nc.scalar, "dve": nc.vector, "pool": nc.gpsimd, "sp": nc.sync}
        self.sem = {e: nc.alloc_semaphore("s_" + e) for e in ("pe", "act", "dve")}
        self.cnt = {e: 0 for e in ("pe", "act", "dve")}
        self.seen = {e: {} for e in self.E}
        self.nwait = 0
        self.nins = 0

    def _waits(self, eng, reads, writes):
        deps = {}
        for b in reads:
            if b.w is not None:
                s, v = b.w
                if deps.get(s, 0) < v:
                    deps[s] = v
        for b in writes:
            if b.w is not None:
                s, v = b.w
                if deps.get(s, 0) < v:
                    deps[s] = v
            for s, v in b.r.items():
                if deps.get(s, 0) < v:
                    deps[s] = v
        E = self.E[eng]
        seen = self.seen[eng]
        for s, v in deps.items():
            if eng == "pe" and s is self.sem["pe"]:
                continue
            if seen.get(s, 0) < v:
                E.wait_ge(s, v)
                seen[s] = v
                self.nwait += 1

    def _record(self, ev, reads, writes):
        s, v = ev
        for b in reads:
            if b.r.get(s, 0) < v:
                b.r[s] = v
        for b in writes:
            b.w = ev
            b.r = {}

    def op(self, eng, fn, reads=(), writes=(), inc=True):
        self._waits(eng, reads, writes)
        ins = fn(self.E[eng])
        self.nins += 1
        if inc:
            self.cnt[eng] += 1
            ins.then_inc(self.sem[eng], 1)
            ev = (self.sem[eng], self.cnt[eng])
        else:
            ev = (self.sem[eng], self.cnt[eng] + 1)
        self._record(ev, reads, writes)

    def dma(self, eng, out, in_, reads=(), writes=(), dst=None, ev_override=None):
        self._waits(eng, reads, writes)
        ins = self.E[eng].dma_start(out=out, in_=in_)
        self.nins += 1
        r = dst if dst is not None else writes[0]
        if r.dsem is None:
            r.dsem = self.nc.alloc_semaphore("d_" + r.name)
        r.dn += 16
        ins.then_inc(r.dsem, 16)
        ev = ev_override if ev_override is not None else (r.dsem, r.dn)
        self._record(ev, reads, writes)
        return ev


def na_live_rows(qb, tl):
    live = []
    for j in range(8):
        qr = 8 * qb + j
        rs = min(max(qr - 4, 0), 24)
        if rs <= 2 * tl + 1 and 2 * tl <= rs + 7:
            live.append(j)
    assert live and live == list(range(live[0], live[-1] + 1))
    return live[0], live[-1]


def blocks5():
    return [(i * 512, 512) for i in range(4)] + [(S, L)]


def build_program(nb, nlayers, last_stage, debug_ctx=False):
    nc = bass.Bass("TRN2", target_bir_lowering=False)
    k = K(nc)

    def din(name, shape, dt=F32):
        return nc.dram_tensor(name, list(shape), dt, kind="ExternalInput").ap()

    xT_d = din("xT", [nb, 128, NCH, T])
    cT_d = din("cT", [128, NCH, 3])
    wada_d = din("w_ada", [DEPTH * 48, 128, NCH * 128])
    bada_d = din("b_adaT", [128, DEPTH * 48])
    lng_d = din("ln_gT", [128, DEPTH * 2 * NCH])
    lnb_d = din("ln_bT", [128, DEPTH * 2 * NCH])
    winab_d = din("w_in_ab", [2 * 24, 128, NCH * 128])
    woab_d = din("w_o_ab", [2 * 8, 128, NCH * 128])
    winc_d = din("w_in_c", [2 * 12, 128, NCH * 128])
    woc_d = din("w_o_c", [2 * 8, 128, NCH * 128])
    wup_d = din("w_up", [DEPTH * 44, 128, NCH * 128])
    wdn_d = din("w_down", [DEPTH * 8, 128, NJ * 128])
    conv_d = din("convT", [128, DEPTH * 44 * 4])
    nab_d = din("na_bias", [2 * 8 * NA_NT, 128, 512])
    cos_d = din("cosT", [128, S])
    sin_d = din("sinT", [128, S])
    perm_d = din("permT", [128, 128])
    lam_d = din("lamv", [128, 2 * 256])
    sub_d = din("sublnT", [128, 2])
    qkg_d = din("qkgT", [128, 4])
    out_d = nc.dram_tensor("outT", [nb, 128, NCH, T if debug_ctx else S], F32, kind="ExternalOutput").ap()

    def sb(name, shape, dt):
        return nc.alloc_sbuf_tensor(name, list(shape), dt)

    XT = sb("XT", [128, NCH, T], F32)
    HT = sb("HT", [128, NCH, T], BF16)
    OT = sb("OT", [128, NCH * T], BF16)
    QKV = sb("QKV", [128, 4 * T], BF16)
    QU = QKV[:, 0:T]
    KU = QKV[:, T:2 * T]
    VA = QKV[:, 2 * T:4 * T].rearrange("p (t s d) -> p t s d", t=18, s=2)
    COS = sb("COS", [128, S], F32)
    SIN = sb("SIN", [128, S], F32)
    NTMP = 6
    TMP = [sb(f"TMP{i}", [128, 512], F32) for i in range(NTMP)]
    BADA = TMP[4]
    LAMV = TMP[3]
    NPT = 2
    PT = [sb(f"PT{i}", [128, 1024], BF16) for i in range(NPT)]
    NW = 4
    WT = [sb(f"WT{i}", [128, NCH, 128], BF16) for i in range(NW)]
    NWD = 2
    WD = [QKV[:, 0:NJ * 128].rearrange("p (j o) -> p j o", j=NJ),
          QKV[:, 2 * T:2 * T + NJ * 128].rearrange("p (j o) -> p j o", j=NJ)]
    MOD = sb("MOD", [128, DEPTH * 6 * NCH * 3], F32)
    LNG = sb("LNG", [128, DEPTH * 2 * NCH], F32)
    LNB = sb("LNB", [128, DEPTH * 2 * NCH], F32)
    CONV = sb("CONV", [128, 44 * 4], F32)
    CT = sb("CT", [128, NCH, 3], F32)
    CONDT = sb("CONDT", [128, NCH, 3], F32)
    PERM = sb("PERM", [128, 128], F32)
    ONESF = sb("ONESF", [128, 128], F32)
    BLK1 = sb("BLK1", [128, 128], F32)
    ONESB = sb("ONESB", [128, 128], BF16)
    SUBG = sb("SUBG", [128, 2], F32)
    QKG = sb("QKG", [128, 4], F32)
    SM = sb("SM", [128, 64], F32)
    PSA = nc.alloc_psum_tensor("PSA", [128, 1024], F32)
    PSB = nc.alloc_psum_tensor("PSB", [128, 1024], F32)
    PS = [PSA[:, 0:512], PSA[:, 512:1024], PSB[:, 0:512], PSB[:, 512:1024]] + \
         [nc.alloc_psum_tensor(f"PS{i}", [128, 512], F32) for i in range(4, 8)]

    rXT = [Res(f"XT{c}") for c in range(NCH)]
    rHT = [Res(f"HT{c}") for c in range(NCH)]
    rOT = [Res(f"OT{c}") for c in range(NCH)]
    rQU, rKU, rVA = Res("QU"), Res("KU"), Res("VA")
    rTMP = [Res(f"TMP{i}") for i in range(NTMP)]
    rPT = [Res(f"PT{i}") for i in range(NPT)]
    rWT = [Res(f"WT{i}") for i in range(NW)]
    rCONV = Res("CONV")
    rPS = [Res(f"PS{i}") for i in range(8)]
    rMOD, rCONST, rSM = Res("MOD"), Res("CONST"), Res("SM")
    rST = Res("STORE")
    ctr = {"tmp": 0, "pt": 0, "wt": 0, "wd": 0, "g": 0, "p": 0}

    def tmp():
        i = ctr["tmp"] % NTMP
        ctr["tmp"] += 1
        return TMP[i], rTMP[i]

    def ptile():
        i = ctr["pt"] % NPT
        ctr["pt"] += 1
        return PT[i], rPT[i]

    def gbank():
        i = 4 + ctr["g"] % 4
        ctr["g"] += 1
        return PS[i], rPS[i]

    def load_w(src_ap):
        i = ctr["wt"] % NW
        ctr["wt"] += 1
        k.dma("pool", WT[i][:].rearrange("p k o -> p (k o)"), src_ap, writes=[rWT[i]])
        return WT[i], rWT[i]

    for dst, src in ((LNG, lng_d), (LNB, lnb_d), (COS, cos_d), (SIN, sin_d),
                     (PERM, perm_d), (SUBG, sub_d), (QKG, qkg_d)):
        k.dma("sp", dst[:], src, writes=[rCONST])
    k.dma("sp", BADA[:, 0:DEPTH * 48], bada_d, writes=[rTMP[4]])
    k.dma("sp", LAMV[:, 0:512], lam_d, writes=[rTMP[3]])
    k.dma("sp", CT[:].rearrange("p k r -> p (k r)"), cT_d.rearrange("p k r -> p (k r)"), writes=[rCONST])
    k.op("dve", lambda e: e.memset(ONESF[:], 1.0), writes=[rCONST])
    k.op("dve", lambda e: e.memset(ONESB[:], 1.0), writes=[rCONST])
    k.op("dve", lambda e: e.memset(BLK1[:], 0.0), writes=[rCONST])
    k.op("dve", lambda e: e.memset(BLK1[0:64, 0:64], 1.0), writes=[rCONST])
    k.op("dve", lambda e: e.memset(BLK1[64:128, 64:128], 1.0), writes=[rCONST])
    k.op("dve", lambda e: e.tensor_scalar(out=QKG[:], in0=QKG[:], scalar1=8.0, scalar2=None, op0=ALU.mult),
         reads=[rCONST], writes=[rCONST])
    k.op("act", lambda e: e.activation(out=CONDT[:].rearrange("p k r -> p (k r)"),
                                       in_=CT[:].rearrange("p k r -> p (k r)"), func=AF.Silu),
         reads=[rCONST], writes=[rCONST])

    def midx(l, m, c, r):
        return ((l * 6 + m) * NCH + c) * 3 + r

    def modap(l, m, c, r):
        i = midx(l, m, c, r)
        return MOD[:, i:i + 1]

    for l in range(nlayers):
        for og in range(3):
            ps, rps = gbank()
            for j in range(16):
                oc = og * 16 + j
                c = oc % NCH
                wa = XT[:, c, 0:1024]
                k.dma("sp", wa, wada_d[l * 48 + oc], writes=[rXT[c]])
                wav = wa.rearrange("p (k o) -> p k o", k=NCH)
                for kc in range(NCH):
                    k.op("pe", lambda e, kc=kc, wav=wav, ps=ps, j=j: e.matmul(
                        ps[:, 3 * j:3 * j + 3], wav[:, kc, :], CONDT[:, kc, :], start=(kc == 0), stop=(kc == NCH - 1)),
                        reads=[rXT[c], rCONST], writes=[rps], inc=(kc == NCH - 1))
            for j in range(16):
                oc = og * 16 + j
                i0 = (l * 48 + oc) * 3
                k.op("dve", lambda e, ps=ps, j=j, i0=i0, oc=oc: e.tensor_scalar(
                    out=MOD[:, i0:i0 + 3], in0=ps[:, 3 * j:3 * j + 3], scalar1=BADA[:, l * 48 + oc:l * 48 + oc + 1],
                    scalar2=None, op0=ALU.add), reads=[rps, rTMP[4]], writes=[rMOD])
        for m, (op_, val) in ((1, (ALU.add, 1.0)), (4, (ALU.add, 1.0)), (2, (ALU.mult, 1.0 / ALPHA)),
                              (5, (ALU.mult, 1.0 / ALPHA))):
            a = midx(l, m, 0, 0)
            k.op("dve", lambda e, a=a, op_=op_, val=val: e.tensor_scalar(
                out=MOD[:, a:a + 24], in0=MOD[:, a:a + 24], scalar1=val, scalar2=None, op0=op_),
                reads=[rMOD], writes=[rMOD])

    for i in range(2):
        l = 2 * i
        if l >= nlayers:
            continue
        lam_init = 0.8 - 0.6 * math.exp(-0.3 * l)
        t0, rt0 = tmp()
        for a in range(2):
            k.op("dve", lambda e, a=a: e.tensor_tensor(
                out=t0[:, a * 64:(a + 1) * 64], in0=LAMV[:, i * 256 + a * 128:i * 256 + a * 128 + 64],
                in1=LAMV[:, i * 256 + a * 128 + 64:i * 256 + a * 128 + 128], op=ALU.mult),
                reads=[rTMP[3]], writes=[rt0])
            k.op("dve", lambda e, a=a: e.reduce_sum(out=SM[:, i * 8 + 1 + a:i * 8 + 2 + a],
                                                    in_=t0[:, a * 64:(a + 1) * 64], axis=AX.X),
                 reads=[rt0], writes=[rSM])
        k.op("act", lambda e: e.activation(out=SM[:, i * 8 + 3:i * 8 + 5], in_=SM[:, i * 8 + 1:i * 8 + 3], func=AF.Exp),
             reads=[rSM], writes=[rSM])
        k.op("dve", lambda e: e.tensor_tensor(out=SM[:, i * 8:i * 8 + 1], in0=SM[:, i * 8 + 4:i * 8 + 5],
                                              in1=SM[:, i * 8 + 3:i * 8 + 4], op=ALU.subtract),
             reads=[rSM], writes=[rSM])
        k.op("dve", lambda e: e.tensor_scalar(out=SM[:, i * 8:i * 8 + 1], in0=SM[:, i * 8:i * 8 + 1],
                                              scalar1=-lam_init, scalar2=None, op0=ALU.add),
             reads=[rSM], writes=[rSM])
        k.op("dve", lambda e: e.tensor_scalar(out=SUBG[:, i:i + 1], in0=SUBG[:, i:i + 1], scalar1=1.0 - lam_init,
                                              scalar2=None, op0=ALU.mult), reads=[rCONST], writes=[rCONST])

    def modulate_pass(l, m_shift, m_scale, b, with_ctx):
        for c in range(NCH):
            k.op("act", lambda e, c=c: e.activation(out=HT[:, c, 0:S], in_=XT[:, c, 0:S], func=AF.Identity,
                                                    bias=modap(l, m_shift, c, b), scale=modap(l, m_scale, c, b)),
                 reads=[rXT[c], rMOD], writes=[rHT[c]])
            if with_ctx:
                k.op("act", lambda e, c=c: e.activation(out=HT[:, c, S:T], in_=XT[:, c, S:T], func=AF.Identity,
                                                        bias=modap(l, m_shift, c, 2), scale=modap(l, m_scale, c, 2)),
                     reads=[rXT[c], rMOD], writes=[rHT[c]])

    def rstd_from(ps_ap, n, scale, bias, out_t, rout, rin):
        k.op("act", lambda e: e.activation(out=out_t[:, :n], in_=ps_ap, func=AF.Ln, bias=bias, scale=scale),
             reads=rin, writes=[rout])
        k.op("act", lambda e: e.activation(out=out_t[:, :n], in_=out_t[:, :n], func=AF.Exp, scale=-0.5),
             reads=[rout], writes=[rout])

    def pbank():
        i = ctr["p"] % 8
        ctr["p"] += 1
        return PS[i], rPS[i]

    def proj_fm(w_t, rw, mode, dst_t, rdst, g8=None, half=None):
        blks = blocks5()
        st = {}

        def stage1(bi):
            c0, n = blks[bi]
            ps, rps = pbank()
            if half is None:
                for kc in range(NCH):
                    k.op("pe", lambda e, kc=kc: e.matmul(ps[:, :n], w_t[:, kc, :], HT[:, kc, c0:c0 + n],
                                                         start=(kc == 0), stop=(kc == NCH - 1)),
                         reads=[rw, rHT[kc]], writes=[rps], inc=(kc == NCH - 1))
            else:
                for hh in range(2):
                    for kc in range(NCH):
                        k.op("pe", lambda e, kc=kc, hh=hh: e.matmul(
                            ps[hh * 64:(hh + 1) * 64, :n], w_t[:, kc, half * 64:(half + 1) * 64], HT[:, kc, c0:c0 + n],
                            start=(kc == 0), stop=(kc == NCH - 1), tile_position=(0, hh * 64)),
                            reads=[rw, rHT[kc]], writes=[rps], inc=(kc == NCH - 1 and hh == 1))
            if mode == "plain" or (mode == "rope" and c0 >= S):
                k.op("act", lambda e: e.activation(out=dst_t[:, c0:c0 + n], in_=ps[:, :n], func=AF.Copy),
                     reads=[rps], writes=[rdst])
                st[bi] = None
                return
            qf, rqf = tmp()
            d = dict(qf=qf, rqf=rqf, n=n, c0=c0)
            if mode == "gqa":
                k.op("act", lambda e: e.activation(out=qf[:, :n], in_=ps[:, :n], func=AF.Identity, scale=g8),
                     reads=[rps, rCONST], writes=[rqf])
                sq, rsq = tmp()
                k.op("act", lambda e: e.activation(out=sq[:, :n], in_=ps[:, :n], func=AF.Square),
                     reads=[rps], writes=[rsq])
                d.update(sq=sq, rsq=rsq)
            else:
                k.op("act", lambda e: e.activation(out=qf[:, :n], in_=ps[:, :n], func=AF.Copy),
                     reads=[rps], writes=[rqf])
            st[bi] = d

        def stage2(bi):
            d = st[bi]
            if d is None:
                return
            n = d["n"]
            if mode == "gqa":
                ss, rss = pbank()
                k.op("pe", lambda e: e.matmul(ss[:, :n], BLK1[:], d["sq"][:, :n], start=True, stop=True),
                     reads=[d["rsq"], rCONST], writes=[rss])
                d.update(ss=ss, rss=rss)
            if d["c0"] < S:
                rot, rrot = pbank()
                k.op("pe", lambda e: e.matmul(rot[:, :n], PERM[:], d["qf"][:, :n], start=True, stop=True),
                     reads=[d["rqf"], rCONST], writes=[rrot])
                d.update(rot=rot, rrot=rrot)

        def stage3(bi):
            d = st[bi]
            if d is None:
                return
            n, c0, qf, rqf = d["n"], d["c0"], d["qf"], d["rqf"]
            dst_ap = dst_t[:, c0:c0 + n]
            if mode == "gqa":
                ss, rss = d["ss"], d["rss"]
                R, rR = tmp()
                k.op("act", lambda e: e.activation(out=R[:, :n], in_=ss[:, :n], func=AF.Ln, bias=64.0 * EPS, scale=1.0),
                     reads=[rss], writes=[rR])
                k.op("act", lambda e: e.activation(out=R[:, :n], in_=R[:, :n], func=AF.Exp, scale=-0.5),
                     reads=[rR], writes=[rR])
                d.update(R=R, rR=rR)
            if c0 < S:
                if mode == "gqa":
                    B, rB = d["sq"], d["rsq"]
                else:
                    B, rB = tmp()
                k.op("dve", lambda e: e.tensor_tensor(out=B[:, :n], in0=d["rot"][:, :n], in1=SIN[:, c0:c0 + n], op=ALU.mult),
                     reads=[d["rrot"], rCONST], writes=[rB])
                k.op("dve", lambda e: e.tensor_tensor(out=qf[:, :n], in0=qf[:, :n], in1=COS[:, c0:c0 + n], op=ALU.mult),
                     reads=[rqf, rCONST], writes=[rqf])
                if mode == "gqa":
                    k.op("dve", lambda e: e.tensor_tensor(out=qf[:, :n], in0=qf[:, :n], in1=B[:, :n], op=ALU.add),
                         reads=[rqf, rB], writes=[rqf])
                    k.op("dve", lambda e: e.tensor_tensor(out=dst_ap, in0=qf[:, :n], in1=d["R"][:, :n], op=ALU.mult),
                         reads=[rqf, d["rR"]], writes=[rdst])
                else:
                    k.op("dve", lambda e: e.tensor_tensor(out=dst_ap, in0=qf[:, :n], in1=B[:, :n], op=ALU.add),
                         reads=[rqf, rB], writes=[rdst])
            else:
                k.op("dve", lambda e: e.tensor_tensor(out=dst_ap, in0=qf[:, :n], in1=d["R"][:, :n], op=ALU.mult),
                     reads=[rqf, d["rR"]], writes=[rdst])

        nblk = len(blks)
        for it in range(nblk + 1):
            if it < nblk:
                stage1(it)
            if it >= 1:
                stage2(it - 1)
                stage3(it - 1)

    def proj_v(w_t, rw, wc0, ncols, dst_fn):
        for t0 in range(0, 18, 4):
            nt = min(4, 18 - t0)
            ps, rps = gbank()
            for j in range(nt):
                t = t0 + j
                for kc in range(NCH):
                    k.op("pe", lambda e, kc=kc, t=t, j=j: e.matmul(
                        ps[:, j * ncols:(j + 1) * ncols], HT[:, kc, t * 128:(t + 1) * 128], w_t[:, kc, wc0:wc0 + ncols],
                        start=(kc == 0), stop=(kc == NCH - 1)),
                        reads=[rw, rHT[kc]], writes=[rps], inc=(kc == NCH - 1 and j == nt - 1))
            dst_fn(ps, rps, t0, nt)

    def attend_seq(jobs, mode, bias_fns=None):
        steps = [(ji, t) for ji, jb in enumerate(jobs) for t in range(len(jb["kt"]))]
        st = {}
        pts = {}
        deferred = []

        def job_state(ji):
            if ji not in st:
                d = dict(accs=[gbank() for _ in range(2 if mode == "aug" else 4)])
                if mode == "diff":
                    d["dacc"] = tmp()
                st[ji] = d
            return st[ji]

        for g in range(len(steps) + 1):
            if g < len(steps):
                ji, t = steps[g]
                jb = jobs[ji]
                c0, nq = jb["c0"], jb["nq"]
                tok0, vaps, tid = jb["kt"][t][:3]
                q0, qn = jb["kt"][t][3:5] if len(jb["kt"][t]) > 3 else (0, nq)
                p = g % 2
                SP = PSA if p == 0 else PSB
                rS = [rPS[2 * p], rPS[2 * p + 1]]
                for s_ in range(2):
                    k.op("pe", lambda e, tok0=tok0, SP=SP, s_=s_, c0=c0, q0=q0, qn=qn: e.matmul(
                        SP[:, s_ * 512 + q0:s_ * 512 + q0 + qn], KU[64 * s_:64 * s_ + 64, tok0:tok0 + 128],
                        QU[64 * s_:64 * s_ + 64, c0 + q0:c0 + q0 + qn], start=True, stop=True),
                        reads=[rKU, rQU], writes=[rS[s_]], inc=(s_ == 1))
                P, rP = ptile()
                if bias_fns is not None and tid is not None:
                    for s_ in range(2):
                        bt, rbt = tmp()
                        k.dma("sp", bt[:, :qn], bias_fns[s_](tid)[:, q0:q0 + qn], writes=[rbt])
                        k.op("dve", lambda e, SP=SP, bt=bt, s_=s_, q0=q0, qn=qn: e.scalar_tensor_tensor(
                            out=bt[:, :qn], in0=SP[:, s_ * 512 + q0:s_ * 512 + q0 + qn], scalar=SCALE, in1=bt[:, :qn],
                            op0=ALU.mult, op1=ALU.add), reads=[rS[s_], rbt], writes=[rbt])
                        k.op("act", lambda e, bt=bt, P=P, s_=s_, q0=q0, qn=qn: e.activation(
                            out=P[:, s_ * 512 + q0:s_ * 512 + q0 + qn], in_=bt[:, :qn], func=AF.Exp),
                            reads=[rbt], writes=[rP])
                else:
                    if nq == 512:
                        src, dst = SP[:, 0:1024], P[:, 0:1024]
                    else:
                        src = SP[:, 0:1024].rearrange("p (s n) -> p s n", s=2)[:, :, 0:nq]
                        dst = P[:, 0:1024].rearrange("p (s n) -> p s n", s=2)[:, :, 0:nq]
                    k.op("act", lambda e, src=src, dst=dst: e.activation(out=dst, in_=src, func=AF.Exp, scale=SCALE),
                         reads=rS, writes=[rP])
                pts[g] = (P, rP)
            if g >= 1:
                ji, tt = steps[g - 1]
                jb = jobs[ji]
                c0, nq = jb["c0"], jb["nq"]
                nt = len(jb["kt"])
                tok0, vaps, tid = jb["kt"][tt][:3]
                q0, qn = jb["kt"][tt][3:5] if len(jb["kt"][tt]) > 3 else (0, nq)
                P, rP = pts.pop(g - 1)
                last = (tt == nt - 1)
                d = job_state(ji)
                accs = d["accs"]
                Os = [accs[0], accs[1]] if mode == "aug" else [accs[0], accs[2]]
                for s_ in range(2):
                    O, rO = Os[s_]
                    k.op("pe", lambda e, vap=vaps[s_], P=P, O=O, s_=s_, q0=q0, qn=qn, tt=tt, last=last: e.matmul(
                        O[:, q0:q0 + qn], vap, P[:, s_ * 512 + q0:s_ * 512 + q0 + qn], start=(tt == 0), stop=last,
                        skip_group_check=True),
                        reads=[rVA, rP], writes=[rO], inc=last)
                if mode == "diff":
                    A, rA = d["dacc"]
                    if tt == 0:
                        k.op("dve", lambda e, P=P, A=A, nq=nq: e.tensor_copy(out=A[:, :nq], in_=P[:, 0:nq]),
                             reads=[rP], writes=[rA])
                    else:
                        k.op("dve", lambda e, P=P, A=A, nq=nq: e.tensor_tensor(
                            out=A[:, :nq], in0=A[:, :nq], in1=P[:, 0:nq], op=ALU.add), reads=[rP, rA], writes=[rA])
                    D1, rD1 = accs[3]
                    k.op("pe", lambda e, P=P, D1=D1, nq=nq, tt=tt, last=last: e.matmul(
                        D1[:, :nq], ONESB[:], P[:, 512:512 + nq], start=(tt == 0), stop=last),
                        reads=[rCONST, rP], writes=[rD1], inc=last)
                    if last:
                        D0, rD0 = accs[1]
                        k.op("pe", lambda e, A=A, D0=D0, nq=nq: e.matmul(D0[:, :nq], ONESF[:], A[:, :nq],
                                                                         start=True, stop=True),
                             reads=[rCONST, rA], writes=[rD0])
                if last:
                    later = jb["fin"](accs, nq, c0)
                    if later is not None:
                        deferred.append((g + later[0], later[1]))
            while deferred and (deferred[0][0] <= g or g == len(steps)):
                deferred.pop(0)[1]()

    def recip_act(dst_ap, src_ap, rsrc, rdst):
        k.op("act", lambda e: e.activation(out=dst_ap, in_=src_ap, func=AF.Ln), reads=rsrc, writes=[rdst])
        k.op("act", lambda e: e.activation(out=dst_ap, in_=dst_ap, func=AF.Exp, scale=-1.0), reads=[rdst], writes=[rdst])

    def aug_fin(ot_chunk, act_recip):
        def fin(accs, nq, c0):
            base = ot_chunk * T + c0
            rds = []
            for s_ in range(2):
                O, rO = accs[s_]
                rd, rrd = tmp()
                if act_recip:
                    recip_act(rd[64:128, :nq], O[64:128, :nq], [rO], rrd)
                else:
                    k.op("dve", lambda e, O=O, rd=rd: e.reciprocal(out=rd[64:128, :nq], in_=O[64:128, :nq]),
                         reads=[rO], writes=[rrd])
                rds.append((rd, rrd))

            def mults():
                for s_ in range(2):
                    O, rO = accs[s_]
                    rd, rrd = rds[s_]
                    k.op("dve", lambda e, O=O, rd=rd, s_=s_: e.tensor_tensor(
                        out=OT[64 * s_:64 * s_ + 64, base:base + nq], in0=O[0:64, :nq], in1=rd[64:128, :nq],
                        op=ALU.mult), reads=[rO, rrd], writes=[rOT[ot_chunk]])
            if act_recip:
                return (2, mults)
            mults()
            return None
        return fin

    def ctx_tiles(vfn0, vfn1):
        return [(S + j * 128, (vfn0(16 + j), vfn1(16 + j)), None) for j in range(2)]

    def res_ln_block(l, which, r, c0, n, ybank_fn, st1, rst1, st2, rst2):
        mg = 2 if which == 0 else 5
        for oc in range(NCH):
            yp, ryp = ybank_fn(oc)
            xs = XT[:, oc, c0:c0 + n]
            k.op("dve", lambda e, yp=yp, xs=xs, oc=oc: e.scalar_tensor_tensor(
                out=xs, in0=yp, scalar=modap(l, mg, oc, r), in1=xs, op0=ALU.mult, op1=ALU.add),
                reads=[ryp, rXT[oc], rMOD], writes=[rXT[oc]])
            sq, rsq = tmp()
            k.op("act", lambda e, xs=xs, sq=sq: e.activation(out=sq[:, :n], in_=xs, func=AF.Square),
                 reads=[rXT[oc]], writes=[rsq])
            k.op("pe", lambda e, xs=xs, oc=oc: e.matmul(st1[:, :n], ONESF[:], xs, start=(oc == 0), stop=(oc == NCH - 1)),
                 reads=[rXT[oc], rCONST], writes=[rst1], inc=False)
            k.op("pe", lambda e, sq=sq, oc=oc: e.matmul(st2[:, :n], ONESF[:], sq[:, :n], start=(oc == 0),
                                                        stop=(oc == NCH - 1)),
                 reads=[rsq, rCONST], writes=[rst2])
        mean, rmean = tmp()
        k.op("dve", lambda e: e.tensor_scalar(out=mean[:, :n], in0=st1[:, :n], scalar1=1.0 / D, scalar2=None,
                                              op0=ALU.mult), reads=[rst1], writes=[rmean])
        var, rvar = tmp()
        k.op("dve", lambda e: e.tensor_tensor(out=var[:, :n], in0=mean[:, :n], in1=mean[:, :n], op=ALU.mult),
             reads=[rmean], writes=[rvar])
        k.op("dve", lambda e: e.scalar_tensor_tensor(out=var[:, :n], in0=st2[:, :n], scalar=1.0 / D, in1=var[:, :n],
                                                     op0=ALU.mult, op1=ALU.subtract),
             reads=[rst2, rvar], writes=[rvar])
        rstd_from(var[:, :n], n, 1.0, EPS_LN, var, rvar, [rvar])
        k.op("dve", lambda e: e.scalar_tensor_tensor(out=mean[:, :n], in0=mean[:, :n], scalar=-1.0, in1=var[:, :n],
                                                     op0=ALU.mult, op1=ALU.mult),
             reads=[rmean, rvar], writes=[rmean])
        gi = (l * 2 + which) * NCH
        for oc in range(NCH):
            xs = XT[:, oc, c0:c0 + n]
            k.op("dve", lambda e, xs=xs: e.tensor_tensor(out=xs, in0=xs, in1=var[:, :n], op=ALU.mult),
                 reads=[rXT[oc], rvar], writes=[rXT[oc]])
            k.op("dve", lambda e, xs=xs: e.tensor_tensor(out=xs, in0=xs, in1=mean[:, :n], op=ALU.add),
                 reads=[rXT[oc], rmean], writes=[rXT[oc]])
            k.op("act", lambda e, xs=xs, oc=oc: e.activation(out=xs, in_=xs, func=AF.Identity,
                                                             bias=LNB[:, gi + oc:gi + oc + 1],
                                                             scale=LNG[:, gi + oc:gi + oc + 1]),
                 reads=[rXT[oc], rCONST], writes=[rXT[oc]])

    def mixer_even(l, b, need_ctx):
        i = l // 2
        k.op("dve", lambda e: e.memset(VA[:, :, :, 64:128], 1.0), writes=[rVA])
        qblocks = [(qb * 512, 512) for qb in range(4)] + ([(S, L)] if need_ctx else [])
        for u in range(4):
            wq, rwq = load_w(winab_d[i * 24 + u])
            proj_fm(wq, rwq, "plain", QU, rQU)
            wk, rwk = load_w(winab_d[i * 24 + 4 + u])
            proj_fm(wk, rwk, "plain", KU, rKU)
            wv, rwv = load_w(winab_d[i * 24 + 8 + u])

            def vdst(ps, rps, t0, nt):
                k.op("act", lambda e: e.activation(
                    out=VA[:, t0:t0 + nt, :, 0:64],
                    in_=ps[:, 0:nt * 128].rearrange("p (t s d) -> p t s d", t=nt, s=2), func=AF.Copy),
                    reads=[rps], writes=[rVA])
            proj_v(wv, rwv, 0, 128, vdst)
            vfn0 = lambda t: VA[:, t, 0, :]
            vfn1 = lambda t: VA[:, t, 1, :]
            bias_fns = tuple((lambda tid, h=2 * u + s_: nab_d[(i * 8 + h) * NA_NT + tid]) for s_ in range(2))
            jobs = []
            for qi, (c0, nq) in enumerate(qblocks):
                kt = ctx_tiles(vfn0, vfn1)
                if c0 < S:
                    for j, tl in enumerate(NA_TILES[qi]):
                        jl, jh = na_live_rows(qi, tl)
                        kt.append((tl * 128, (vfn0(tl), vfn1(tl)), NA_TID0[qi] + j, jl * 64, (jh - jl + 1) * 64))
                jobs.append(dict(c0=c0, nq=nq, kt=kt, fin=aug_fin(u, True)))
            attend_seq(jobs, "aug", bias_fns)
        for u in range(4):
            wq, rwq = load_w(winab_d[i * 24 + 12 + u])
            proj_fm(wq, rwq, "rope", QU, rQU)
            wk, rwk = load_w(winab_d[i * 24 + 16 + u])
            proj_fm(wk, rwk, "rope", KU, rKU)
            wv, rwv = load_w(winab_d[i * 24 + 20 + u])

            def vdst(ps, rps, t0, nt):
                k.op("act", lambda e: e.activation(
                    out=VA[:, t0:t0 + nt, 1, :], in_=ps[:, 0:nt * 128].rearrange("p (t d) -> p t d", t=nt),
                    func=AF.Copy), reads=[rps], writes=[rVA])
            proj_v(wv, rwv, 0, 128, vdst)
            vfn = lambda t: VA[:, t, 1, :]
            def diff_fin(accs, nq, c0, u=u):
                (O0, rO0), (D0, rD0), (O1, rO1), (D1, rD1) = accs
                r1, rr1 = tmp()
                recip_act(r1[:, :nq], D0[:, :nq], [rD0], rr1)
                r2, rr2 = tmp()
                recip_act(r2[:, :nq], D1[:, :nq], [rD1], rr2)
                a1, ra1 = r1, rr1
                k.op("dve", lambda e: e.tensor_tensor(out=a1[:, :nq], in0=O0[:, :nq], in1=r1[:, :nq], op=ALU.mult),
                     reads=[rO0, rr1], writes=[ra1])
                k.op("dve", lambda e: e.tensor_tensor(out=r2[:, :nq], in0=O1[:, :nq], in1=r2[:, :nq], op=ALU.mult),
                     reads=[rO1, rr2], writes=[rr2])
                k.op("dve", lambda e: e.scalar_tensor_tensor(out=a1[:, :nq], in0=r2[:, :nq], scalar=SM[:, i * 8:i * 8 + 1],
                                                             in1=a1[:, :nq], op0=ALU.mult, op1=ALU.add),
                     reads=[rr2, ra1, rSM], writes=[ra1])
                sq, rsq = tmp()
                k.op("act", lambda e: e.activation(out=sq[:, :nq], in_=a1[:, :nq], func=AF.Square),
                     reads=[ra1], writes=[rsq])
                ss, rss = gbank()
                k.op("pe", lambda e: e.matmul(ss[:, :nq], ONESF[:], sq[:, :nq], start=True, stop=True),
                     reads=[rsq, rCONST], writes=[rss])
                rstd_from(ss[:, :nq], nq, 1.0 / 128.0, EPS, sq, rsq, [rss])
                base = (4 + u) * T + c0
                k.op("dve", lambda e: e.scalar_tensor_tensor(out=OT[:, base:base + nq], in0=a1[:, :nq],
                                                             scalar=SUBG[:, i:i + 1], in1=sq[:, :nq],
                                                             op0=ALU.mult, op1=ALU.mult),
                     reads=[ra1, rsq, rCONST], writes=[rOT[4 + u]])
            jobs = []
            for (c0, nq) in qblocks:
                if c0 < S:
                    kt = [(tl * 128, (vfn(tl), vfn(tl)), None) for tl in range(18)]
                else:
                    kt = ctx_tiles(vfn, vfn)
                jobs.append(dict(c0=c0, nq=nq, kt=kt, fin=diff_fin))
            attend_seq(jobs, "diff")
        return woab_d, i * 8

    def mixer_odd(l, b, need_ctx):
        i = l // 2
        k.op("dve", lambda e: e.memset(VA[:, :, 0, 64:128], 1.0), writes=[rVA])
        qblocks = [(qb * 512, 512) for qb in range(4)] + ([(S, L)] if need_ctx else [])
        for u in range(8):
            g = u // 2
            wq, rwq = load_w(winc_d[i * 12 + u])
            proj_fm(wq, rwq, "gqa", QU, rQU, g8=QKG[:, 2 * i:2 * i + 1])
            if u % 2 == 0:
                wk, rwk = load_w(winc_d[i * 12 + 8 + g // 2])
                proj_fm(wk, rwk, "gqa", KU, rKU, g8=QKG[:, 2 * i + 1:2 * i + 2], half=g % 2)
                wv, rwv = load_w(winc_d[i * 12 + 10 + g // 2])

                def vdst(ps, rps, t0, nt):
                    k.op("act", lambda e: e.activation(
                        out=VA[:, t0:t0 + nt, 0, 0:64], in_=ps[:, 0:nt * 64].rearrange("p (t d) -> p t d", t=nt),
                        func=AF.Copy), reads=[rps], writes=[rVA])
                proj_v(wv, rwv, (g % 2) * 64, 64, vdst)
            vfn = lambda t: VA[:, t, 0, :]
            jobs = []
            for (c0, nq) in qblocks:
                if c0 < S:
                    kt = [(tl * 128, (vfn(tl), vfn(tl)), None) for tl in range(18)]
                else:
                    kt = ctx_tiles(vfn, vfn)
                jobs.append(dict(c0=c0, nq=nq, kt=kt, fin=aug_fin(u, False)))
            attend_seq(jobs, "aug")
        return woc_d, i * 8

    def out_proj_ln(l, b, need_ctx, wo_d, wo_base):
        blks = blocks5() if need_ctx else blocks5()[:4]
        bigs = [blks[0:2], blks[2:4]] + ([blks[4:5]] if need_ctx else [])
        for big in bigs:
            for oc in range(NCH):
                w, rw = load_w(wo_d[wo_base + oc])
                for si, (c0, n) in enumerate(big):
                    r = b if c0 < S else 2
                    st1, rst1, st2, rst2 = PS[2 * si], rPS[2 * si], PS[2 * si + 1], rPS[2 * si + 1]
                    ps, rps = gbank()
                    for kc in range(NCH):
                        k.op("pe", lambda e, kc=kc: e.matmul(ps[:, :n], w[:, kc, :], OT[:, kc * T + c0:kc * T + c0 + n],
                                                             start=(kc == 0), stop=(kc == NCH - 1)),
                             reads=[rw, rOT[kc]], writes=[rps], inc=(kc == NCH - 1))
                    xs = XT[:, oc, c0:c0 + n]
                    k.op("dve", lambda e, xs=xs, r=r: e.scalar_tensor_tensor(
                        out=xs, in0=ps[:, :n], scalar=modap(l, 2, oc, r), in1=xs, op0=ALU.mult, op1=ALU.add),
                        reads=[rps, rXT[oc], rMOD], writes=[rXT[oc]])
                    sq, rsq = tmp()
                    k.op("act", lambda e, xs=xs, sq=sq: e.activation(out=sq[:, :n], in_=xs, func=AF.Square),
                         reads=[rXT[oc]], writes=[rsq])
                    k.op("pe", lambda e, xs=xs: e.matmul(st1[:, :n], ONESF[:], xs, start=(oc == 0), stop=(oc == NCH - 1)),
                         reads=[rXT[oc], rCONST], writes=[rst1], inc=False)
                    k.op("pe", lambda e, sq=sq: e.matmul(st2[:, :n], ONESF[:], sq[:, :n], start=(oc == 0),
                                                         stop=(oc == NCH - 1)),
                         reads=[rsq, rCONST], writes=[rst2])
            for si, (c0, n) in enumerate(big):
                ln_finish(l, 0, c0, n, PS[2 * si], rPS[2 * si], PS[2 * si + 1], rPS[2 * si + 1])

    def ffn(l, b, need_ctx):
        subs = [(0, 410, 0, S), (410, 410, 0, S), (820, 410, 0, S), (1230, 410, 0, S), (1640, 408, 0, S)]
        bigs = [[subs[0], subs[1]], [subs[2], subs[3]], [subs[4]]]
        if need_ctx:
            bigs[2].append((S, L, S, T))
        rACT = rOT
        k.dma("sp", CONV[:], conv_d[:, l * 176:(l + 1) * 176], writes=[rCONV])
        pending = [None]
        for big in bigs:
            for j in range(NJ):
                wa, rwa = load_w(wup_d[l * 44 + j])
                wg, rwg = load_w(wup_d[l * 44 + NJ + j])
                for si, (s0, n, q0, q1) in enumerate(big):
                    lo = 1 if s0 - 1 < q0 else 0
                    hi = n - 1 if s0 + n + 1 > q1 else n
                    ra0 = s0 - 1 + lo
                    ncol = (hi + 2) - lo
                    tts = []
                    for which, (w, rw) in enumerate(((wa, rwa), (wg, rwg))):
                        bi = (j % 2) * 4 + si * 2 + which
                        ps, rps = PS[bi], rPS[bi]
                        for kc in range(NCH):
                            k.op("pe", lambda e, kc=kc, w=w, ps=ps: e.matmul(
                                ps[:, lo:lo + ncol], w[:, kc, :], HT[:, kc, ra0:ra0 + ncol],
                                start=(kc == 0), stop=(kc == NCH - 1)),
                                reads=[rw, rHT[kc]], writes=[rps], inc=(kc == NCH - 1))
                        ci = (which * NJ + j) * 4
                        tt, rtt = tmp()
                        k.op("act", lambda e, ps=ps, tt=tt, ci=ci: e.activation(
                            out=tt[:, :n], in_=ps[:, 1:n + 1], func=AF.Identity, bias=CONV[:, ci + 3:ci + 4],
                            scale=CONV[:, ci + 1:ci + 2]), reads=[rps, rCONV], writes=[rtt])
                        k.op("dve", lambda e, ps=ps, tt=tt, ci=ci: e.scalar_tensor_tensor(
                            out=tt[:, lo:n], in0=ps[:, lo:n], scalar=CONV[:, ci:ci + 1], in1=tt[:, lo:n],
                            op0=ALU.mult, op1=ALU.add), reads=[rps, rtt, rCONV], writes=[rtt])
                        k.op("dve", lambda e, ps=ps, tt=tt, ci=ci: e.scalar_tensor_tensor(
                            out=tt[:, 0:hi], in0=ps[:, 2:hi + 2], scalar=CONV[:, ci + 2:ci + 3], in1=tt[:, 0:hi],
                            op0=ALU.mult, op1=ALU.add), reads=[rps, rtt, rCONV], writes=[rtt])
                        tts.append((tt, rtt))
                    (ta, rta), (tg, rtg) = tts
                    ab = (j * 2 + si) * 410

                    def fin(ta=ta, rta=rta, tg=tg, rtg=rtg, n=n, ab=ab, j=j):
                        k.op("act", lambda e: e.activation(out=tg[:, :n], in_=tg[:, :n], func=AF.Gelu_apprx_tanh),
                             reads=[rtg], writes=[rtg])
                        k.op("dve", lambda e: e.tensor_tensor(out=OT[:, ab:ab + n], in0=ta[:, :n], in1=tg[:, :n],
                                                              op=ALU.mult),
                             reads=[rta, rtg], writes=[rACT[j % NCH]])
                    if pending[0] is not None:
                        pending[0]()
                    pending[0] = fin
            if pending[0] is not None:
                pending[0]()
                pending[0] = None
            ybanks = {}
            for oc in range(NCH):
                iwd = ctr["wd"] % NWD
                ctr["wd"] += 1
                wdres = [rQU, rKU] if iwd == 0 else [rVA]
                wdsem = rQU if iwd == 0 else rVA
                k.dma("pool", (QKV[:, 0:NJ * 128] if iwd == 0 else QKV[:, 2 * T:2 * T + NJ * 128]), wdn_d[l * 8 + oc],
                      writes=wdres, dst=wdsem)
                for si, (s0, n, q0, q1) in enumerate(big):
                    bi = (oc % 2) * 2 + si
                    ps, rps = PS[bi], rPS[bi]
                    for j in range(NJ):
                        ab = (j * 2 + si) * 410
                        k.op("pe", lambda e, j=j, ab=ab, ps=ps, iwd=iwd: e.matmul(
                            ps[:, :n], WD[iwd][:, j, :], OT[:, ab:ab + n], start=(j == 0), stop=(j == NJ - 1)),
                            reads=wdres + [rACT[j % NCH]], writes=[rps], inc=(j == NJ - 1))
                    ybanks[(oc, si)] = (ps, rps)
                    r = b if s0 < S else 2
                    st1, rst1, st2, rst2 = PS[4 + 2 * si], rPS[4 + 2 * si], PS[5 + 2 * si], rPS[5 + 2 * si]
                    xs = XT[:, oc, s0:s0 + n]
                    k.op("dve", lambda e, ps=ps, xs=xs, oc=oc, r=r: e.scalar_tensor_tensor(
                        out=xs, in0=ps[:, :n], scalar=modap(l, 5, oc, r), in1=xs, op0=ALU.mult, op1=ALU.add),
                        reads=[rps, rXT[oc], rMOD], writes=[rXT[oc]])
                    sq, rsq = tmp()
                    k.op("act", lambda e, xs=xs, sq=sq: e.activation(out=sq[:, :n], in_=xs, func=AF.Square),
                         reads=[rXT[oc]], writes=[rsq])
                    k.op("pe", lambda e, xs=xs, oc=oc, st1=st1: e.matmul(st1[:, :n], ONESF[:], xs, start=(oc == 0),
                                                                         stop=(oc == NCH - 1)),
                         reads=[rXT[oc], rCONST], writes=[rst1], inc=False)
                    k.op("pe", lambda e, sq=sq, oc=oc, st2=st2: e.matmul(st2[:, :n], ONESF[:], sq[:, :n], start=(oc == 0),
                                                                         stop=(oc == NCH - 1)),
                         reads=[rsq, rCONST], writes=[rst2])
            for si, (s0, n, q0, q1) in enumerate(big):
                st1, rst1, st2, rst2 = PS[4 + 2 * si], rPS[4 + 2 * si], PS[5 + 2 * si], rPS[5 + 2 * si]
                ln_finish(l, 1, s0, n, st1, rst1, st2, rst2)

    def ln_finish(l, which, c0, n, st1, rst1, st2, rst2):
        mean, rmean = tmp()
        k.op("dve", lambda e: e.tensor_scalar(out=mean[:, :n], in0=st1[:, :n], scalar1=1.0 / D, scalar2=None,
                                              op0=ALU.mult), reads=[rst1], writes=[rmean])
        var, rvar = tmp()
        k.op("dve", lambda e: e.tensor_tensor(out=var[:, :n], in0=mean[:, :n], in1=mean[:, :n], op=ALU.mult),
             reads=[rmean], writes=[rvar])
        k.op("dve", lambda e: e.scalar_tensor_tensor(out=var[:, :n], in0=st2[:, :n], scalar=1.0 / D, in1=var[:, :n],
                                                     op0=ALU.mult, op1=ALU.subtract),
             reads=[rst2, rvar], writes=[rvar])
        rstd_from(var[:, :n], n, 1.0, EPS_LN, var, rvar, [rvar])
        k.op("dve", lambda e: e.scalar_tensor_tensor(out=mean[:, :n], in0=mean[:, :n], scalar=-1.0, in1=var[:, :n],
                                                     op0=ALU.mult, op1=ALU.mult),
             reads=[rmean, rvar], writes=[rmean])
        gi = (l * 2 + which) * NCH
        for oc in range(NCH):
            xs = XT[:, oc, c0:c0 + n]
            k.op("dve", lambda e, xs=xs: e.tensor_tensor(out=xs, in0=xs, in1=var[:, :n], op=ALU.mult),
                 reads=[rXT[oc], rvar], writes=[rXT[oc]])
            k.op("dve", lambda e, xs=xs: e.tensor_tensor(out=xs, in0=xs, in1=mean[:, :n], op=ALU.add),
                 reads=[rXT[oc], rmean], writes=[rXT[oc]])
            k.op("act", lambda e, xs=xs, oc=oc: e.activation(out=xs, in_=xs, func=AF.Identity,
                                                             bias=LNB[:, gi + oc:gi + oc + 1],
                                                             scale=LNG[:, gi + oc:gi + oc + 1]),
                 reads=[rXT[oc], rCONST], writes=[rXT[oc]])

    nstore = 0
    for b in range(nb):
        for c in range(NCH):
            k.dma("sp", XT[:, c, :], xT_d[b, :, c, :], writes=[rXT[c]])
        for l in range(nlayers):
            need_ctx = l < DEPTH - 1
            modulate_pass(l, 0, 1, b, True)
            if l % 2 == 0:
                wo_d, wo_base = mixer_even(l, b, need_ctx)
            else:
                wo_d, wo_base = mixer_odd(l, b, need_ctx)
            out_proj_ln(l, b, need_ctx, wo_d, wo_base)
            if l == nlayers - 1 and last_stage == 1:
                break
            modulate_pass(l, 3, 4, b, need_ctx)
            ffn(l, b, need_ctx)
        nstore += NCH
        ncol_out = T if debug_ctx else S
        for c in range(NCH):
            k.dma("sp", out_d[b, :, c, :], XT[:, c, 0:ncol_out], reads=[rXT[c]], dst=rST,
                  ev_override=None)
        for c in range(NCH):
            rXT[c].r[rST.dsem] = rST.dn
    nc.sync.wait_ge(rST.dsem, rST.dn)
    print(f"[kernel] instructions={k.nins} waits={k.nwait} cnt={k.cnt}", flush=True)
    return nc


def _tile_w(w, ncol_tiles):
    kin = w.shape[0] // 128
    a = w.reshape(kin, 128, ncol_tiles, 128).transpose(2, 1, 0, 3)
    return np.ascontiguousarray(a).reshape(ncol_tiles, 128, kin * 128)


def _fm(v):
    sh = v.shape
    n = sh[-1] // 128
    a = v.reshape(sh[:-1] + (n, 128))
    a = np.moveaxis(a, -1, 0)
    return np.ascontiguousarray(a).reshape(128, -1)


def _na_bias(rpb):
    H = rpb.shape[0]
    c = np.arange(64)
    cs = np.clip(c - 8, 0, 48)
    col_ok = (c[None, :] >= cs[:, None]) & (c[None, :] < cs[:, None] + 16)
    dc = np.clip(c[None, :] - c[:, None] + 15, 0, 30)
    bc = np.where(col_ok[None, None], rpb[:, :, dc], np.float32(NEG)).astype(np.float32)
    bcT = np.ascontiguousarray(bc.transpose(0, 1, 3, 2))
    out = np.full((H, NA_NT, 128, 512), NEG, np.float32)
    for qb in (0, 1, 3):
        for j, tl in enumerate(NA_TILES[qb]):
            tid = NA_TID0[qb] + j
            for ii in range(2):
                kr = 2 * tl + ii
                for jj in range(8):
                    qr = 8 * qb + jj
                    rs = min(max(qr - 4, 0), 24)
                    if rs <= kr < rs + 8:
                        out[:, tid, ii * 64:(ii + 1) * 64, jj * 64:(jj + 1) * 64] = bcT[:, kr - qr + 7]
    return out


def _rope_tables():
    inv = (10000.0 ** (-np.arange(16, dtype=np.float32) / 16)).astype(np.float32)
    t = np.arange(S)
    row = (t // 64).astype(np.float32)
    col = (t % 64).astype(np.float32)
    ar = row[:, None] * inv
    ac = col[:, None] * inv
    ang = np.concatenate([ar, ar, ac, ac], -1).astype(np.float32)
    cos = np.cos(ang).astype(np.float32).T
    sin = np.sin(ang).astype(np.float32).T
    sign = np.ones(64, np.float32)
    sign[0:16] = -1
    sign[32:48] = -1
    sinS = sin * sign[:, None]
    perm = np.zeros((128, 128), np.float32)
    for blk in range(2):
        for d in range(64):
            src = d + 16 if (d < 16 or 32 <= d < 48) else d - 16
            perm[blk * 64 + src, blk * 64 + d] = 1.0
    return (np.ascontiguousarray(np.tile(cos, (2, 1))), np.ascontiguousarray(np.tile(sinS, (2, 1))), perm)


def _shared_inputs(inp):
    f = np.float32
    sh = {}
    sh["w_ada"] = np.concatenate([_tile_w(np.asarray(inp["w_ada"][l], f), 48) for l in range(DEPTH)], 0)
    sh["b_adaT"] = _fm(np.asarray(inp["b_ada"], f).reshape(-1))
    sh["ln_gT"] = _fm(np.asarray(inp["ln_g"], f).reshape(-1))
    sh["ln_bT"] = _fm(np.asarray(inp["ln_b"], f).reshape(-1))
    sh["w_in_ab"] = np.concatenate([_tile_w(np.asarray(inp["w_in_ab"][i], f), 24) for i in range(2)], 0)
    sh["w_o_ab"] = np.concatenate([_tile_w(np.asarray(inp["w_o_ab"][i], f), 8) for i in range(2)], 0)
    sh["w_in_c"] = np.concatenate([_tile_w(np.asarray(inp["w_in_c"][i], f), 12) for i in range(2)], 0)
    sh["w_o_c"] = np.concatenate([_tile_w(np.asarray(inp["w_o_c"][i], f), 8) for i in range(2)], 0)
    sh["w_up"] = np.concatenate([_tile_w(np.asarray(inp["w_up"][l], f), 44) for l in range(DEPTH)], 0)
    sh["w_down"] = np.concatenate([_tile_w(np.asarray(inp["w_down"][l], f), 8) for l in range(DEPTH)], 0)
    cw = np.asarray(inp["conv_w"], f)
    cb = np.asarray(inp["conv_b"], f)
    cv = np.concatenate([cw, cb[:, None, :]], 1)
    cv = cv.reshape(DEPTH, 4, 44, 128).transpose(3, 0, 2, 1)
    sh["convT"] = np.ascontiguousarray(cv).reshape(128, -1)
    sh["na_bias"] = np.concatenate([_na_bias(np.asarray(inp["na_rpb"][i], f)) for i in range(2)], 0).reshape(
        2 * 8 * NA_NT, 128, 512)
    cosT, sinT, perm = _rope_tables()
    sh["cosT"], sh["sinT"], sh["permT"] = cosT, sinT, perm
    lam = np.asarray(inp["diff_lambda"], f).reshape(1, 2 * 256)
    sh["lamv"] = np.ascontiguousarray(np.broadcast_to(lam, (128, 512)))
    sh["sublnT"] = np.ascontiguousarray(np.asarray(inp["diff_subln"], f).T)
    g = np.asarray(inp["gqa_qk_norm"], f).reshape(4, 64)
    sh["qkgT"] = np.ascontiguousarray(np.tile(g.T, (2, 1)))
    return sh


def _core_inputs(inp, core, nb):
    f = np.float32
    x = np.asarray(inp["x"], f)
    ctx = np.asarray(inp["ctx"], f)
    c = np.asarray(inp["c"], f)
    cc = np.asarray(inp["c_ctx"], f)
    bs = [core * 2 + j for j in range(nb)]
    xs = []
    for b_ in bs:
        full = np.concatenate([x[b_], ctx[b_]], 0)
        xs.append(full.T.reshape(NCH, 128, T).transpose(1, 0, 2))
    rows = [c[core * 2], c[core * 2 + 1], cc]
    cT = np.stack(rows, 0).reshape(3, NCH, 128).transpose(2, 1, 0)
    return {"xT": np.ascontiguousarray(np.stack(xs, 0)), "cT": np.ascontiguousarray(cT)}


def run(inputs, nb=2, nlayers=DEPTH, last_stage=2, ncores=8, debug_ctx=False):
    nc = build_program(nb, nlayers, last_stage, debug_ctx)
    sh = _shared_inputs(inputs)
    in_maps = []
    for core in range(ncores):
        m = dict(sh)
        m.update(_core_inputs(inputs, core, nb))
        in_maps.append(m)
    res = run_bass_kernel_spmd(nc, in_maps, core_ids=list(range(ncores)))
    outs = []
    for core in range(ncores):
        o = res.results[core]["outT"]
        ncol = o.shape[-1]
        outs.append(o.transpose(0, 3, 2, 1).reshape(nb, ncol, D))
    return outs


def kernel(**inputs):
    outs = run(inputs)
    return np.ascontiguousarray(np.concatenate(outs, 0)).astype(np.float32)
```
